# Optimizing a Trainium2 kernel written in Bass

```python
import math
import jax
import jax.numpy as jnp
from jax import lax

D_MODEL = 1024
BATCH = 16
SEQ = 2048
DEPTH = 1

LRU_WIDTH = D_MODEL // 2
LRU_HEADS = 8
LRU_BLOCK = LRU_WIDTH // LRU_HEADS
LRU_CONV = 4
LRU_CONV_LEFT = 2
LRU_C = 8.0
N_DIR = 2

MLA_HEADS = 8
QK_NOPE = 64
QK_ROPE = 32
V_HEAD = 64
Q_LORA = D_MODEL // 4
KV_LORA = D_MODEL // 8
ROPE_THETA = 10000.0
Q_BLOCK = 128
MLA_WIDTH = MLA_HEADS * V_HEAD

MIX_WIDTH = LRU_WIDTH + MLA_WIDTH

C_LRU_X = LRU_WIDTH
C_LRU_G = LRU_WIDTH
C_QLAT = Q_LORA
C_KVLAT = KV_LORA
C_KROPE = QK_ROPE
IN_COLS = C_LRU_X + C_LRU_G + C_QLAT + C_KVLAT + C_KROPE

D_FF = 2816
FFN_CONV = 3
FFN_CONV_LEFT = 1

EPS = 1e-6

kernel_name = "hybrid_rglru_mla_convffn_encoder"


def rms_norm(x, g):
    xf = x.astype(jnp.float32)
    y = xf * lax.rsqrt(jnp.mean(xf * xf, axis=-1, keepdims=True) + EPS)
    return (y * g.astype(jnp.float32)).astype(x.dtype)


def depthwise_conv(x, w, b, left):
    k_width = w.shape[0]
    s = x.shape[1]
    xp = jnp.pad(x, ((0, 0), (left, k_width - 1 - left), (0, 0)))
    out = xp[:, 0:s] * w[0]
    for k in range(1, k_width):
        out = out + xp[:, k:k + s] * w[k]
    return out + b


def _lru_combine(left, right):
    a1, b1 = left
    a2, b2 = right
    return a1 * a2, a2 * b1 + b2


def rg_lru(xc, w_a, b_a, w_x, b_x, lam, reverse):
    bsz, s, w = xc.shape
    xf = xc.astype(jnp.float32)
    xh = xf.reshape(bsz, s, LRU_HEADS, LRU_BLOCK)
    r = jax.nn.sigmoid(jnp.einsum('bshi,hij->bshj', xh, w_a.astype(jnp.float32)).reshape(bsz, s, w)
                       + b_a.astype(jnp.float32))
    i = jax.nn.sigmoid(jnp.einsum('bshi,hij->bshj', xh, w_x.astype(jnp.float32)).reshape(bsz, s, w)
                       + b_x.astype(jnp.float32))
    log_a = -LRU_C * r * jax.nn.softplus(-lam.astype(jnp.float32))
    a = jnp.exp(log_a)
    u = jnp.sqrt(-jnp.expm1(2.0 * log_a)) * (i * xf)
    _, h = lax.associative_scan(_lru_combine, (a, u), axis=1, reverse=reverse)
    return h


def rope_tables(s, dtype):
    half = QK_ROPE // 2
    freqs = ROPE_THETA ** (-jnp.arange(half, dtype=jnp.float32) / half)
    ang = jnp.arange(s, dtype=jnp.float32)[:, None] * freqs[None, :]
    return jnp.cos(ang).astype(dtype), jnp.sin(ang).astype(dtype)


def apply_rope(x, cos, sin):
    x1, x2 = jnp.split(x, 2, axis=-1)
    c = cos[None, :, None, :]
    sn = sin[None, :, None, :]
    return jnp.concatenate([x1 * c - x2 * sn, x2 * c + x1 * sn], axis=-1)


def dense_attention_blocked(q, k, v):
    bsz, s, h, dqk = q.shape
    dv = v.shape[-1]
    nb = s // Q_BLOCK
    scale = 1.0 / math.sqrt(dqk)
    q_blocks = q.reshape(bsz, nb, Q_BLOCK, h, dqk).transpose(1, 0, 2, 3, 4)

    def attend(q_blk):
        sc = jnp.einsum('bqhd,bkhd->bhqk', q_blk, k).astype(jnp.float32) * scale
        p = jax.nn.softmax(sc, axis=-1).astype(v.dtype)
        return jnp.einsum('bhqk,bkhd->bqhd', p, v)

    out = lax.map(attend, q_blocks)
    return out.transpose(1, 0, 2, 3, 4).reshape(bsz, s, h * dv)


def hybrid_mixer(xn, w_in, lru_conv_w, lru_conv_b, lru_gate_a_w, lru_gate_a_b,
                 lru_gate_x_w, lru_gate_x_b, lru_lambda, q_norm_g, w_uq, kv_norm_g,
                 w_ukv, grp_norm_lru_g, grp_norm_mla_g, w_o, cos, sin):
    bsz, s, _ = xn.shape
    proj = xn @ w_in
    o1 = C_LRU_X
    o2 = o1 + C_LRU_G
    o3 = o2 + C_QLAT
    o4 = o3 + C_KVLAT
    lru_x = proj[..., :o1]
    lru_g = proj[..., o1:o2]
    q_lat = proj[..., o2:o3]
    kv_lat = proj[..., o3:o4]
    k_rope = proj[..., o4:]

    xc = depthwise_conv(lru_x, lru_conv_w, lru_conv_b, LRU_CONV_LEFT)
    h_sum = rg_lru(xc, lru_gate_a_w[0], lru_gate_a_b[0], lru_gate_x_w[0], lru_gate_x_b[0],
                   lru_lambda[0], reverse=False)
    h_sum = h_sum + rg_lru(xc, lru_gate_a_w[1], lru_gate_a_b[1], lru_gate_x_w[1], lru_gate_x_b[1],
                           lru_lambda[1], reverse=True)
    y_lru = (jax.nn.gelu(lru_g) * h_sum.astype(xn.dtype))

    c_q = rms_norm(q_lat, q_norm_g)
    q = jnp.einsum('bsr,rhd->bshd', c_q, w_uq)
    q_nope = q[..., :QK_NOPE]
    q_pe = apply_rope(q[..., QK_NOPE:], cos, sin)
    c_kv = rms_norm(kv_lat, kv_norm_g)
    kv = jnp.einsum('bsr,rhd->bshd', c_kv, w_ukv)
    k_nope = kv[..., :QK_NOPE]
    v = kv[..., QK_NOPE:]
    k_pe = apply_rope(k_rope[:, :, None, :], cos, sin)
    q_full = jnp.concatenate([q_nope, q_pe], axis=-1)
    k_full = jnp.concatenate([k_nope, jnp.broadcast_to(k_pe, (bsz, s, MLA_HEADS, QK_ROPE))], axis=-1)
    y_mla = dense_attention_blocked(q_full, k_full, v)

    y = jnp.concatenate([rms_norm(y_lru, grp_norm_lru_g), rms_norm(y_mla, grp_norm_mla_g)], axis=-1)
    return y @ w_o


def conv_glu_ffn(xn, w_up, ffn_conv_w, ffn_conv_b, w_down):
    u = xn @ w_up
    u = depthwise_conv(u, ffn_conv_w, ffn_conv_b, FFN_CONV_LEFT)
    g = u[..., :D_FF]
    val = u[..., D_FF:]
    return (jax.nn.silu(g) * val) @ w_down


def setup_inputs(seed: int = 0) -> dict:
    key = jax.random.key(seed)
    ks = jax.random.split(key, 24)
    f32 = jnp.float32

    def nrm(k, shape, scale):
        return jax.random.normal(k, shape, f32) * scale

    def gain(k, shape):
        return 1.0 + 0.02 * jax.random.normal(k, shape, f32)

    u = jax.random.uniform(ks[8], (DEPTH, N_DIR, LRU_WIDTH), f32, minval=0.9, maxval=0.999)
    sgm = u ** (1.0 / LRU_C)
    lru_lambda = jnp.log(sgm) - jnp.log1p(-sgm)

    return {
        "x": jax.random.normal(ks[0], (BATCH, SEQ, D_MODEL), f32),
        "ln_mix_g": gain(ks[1], (DEPTH, D_MODEL)),
        "w_in": nrm(ks[2], (DEPTH, D_MODEL, IN_COLS), D_MODEL ** -0.5),
        "lru_conv_w": nrm(ks[3], (DEPTH, LRU_CONV, LRU_WIDTH), LRU_CONV ** -0.5),
        "lru_conv_b": nrm(ks[4], (DEPTH, LRU_WIDTH), 0.01),
        "lru_gate_a_w": nrm(ks[5], (DEPTH, N_DIR, LRU_HEADS, LRU_BLOCK, LRU_BLOCK), LRU_BLOCK ** -0.5),
        "lru_gate_a_b": nrm(ks[6], (DEPTH, N_DIR, LRU_WIDTH), 0.01),
        "lru_gate_x_w": nrm(ks[7], (DEPTH, N_DIR, LRU_HEADS, LRU_BLOCK, LRU_BLOCK), LRU_BLOCK ** -0.5),
        "lru_gate_x_b": nrm(ks[9], (DEPTH, N_DIR, LRU_WIDTH), 0.01),
        "lru_lambda": lru_lambda,
        "q_norm_g": gain(ks[10], (DEPTH, Q_LORA)),
        "w_uq": nrm(ks[11], (DEPTH, Q_LORA, MLA_HEADS, QK_NOPE + QK_ROPE), Q_LORA ** -0.5),
        "kv_norm_g": gain(ks[12], (DEPTH, KV_LORA)),
        "w_ukv": nrm(ks[13], (DEPTH, KV_LORA, MLA_HEADS, QK_NOPE + V_HEAD), KV_LORA ** -0.5),
        "grp_norm_lru_g": gain(ks[14], (DEPTH, LRU_WIDTH)),
        "grp_norm_mla_g": gain(ks[15], (DEPTH, MLA_WIDTH)),
        "w_o": nrm(ks[16], (DEPTH, MIX_WIDTH, D_MODEL), MIX_WIDTH ** -0.5),
        "ln_ffn_g": gain(ks[17], (DEPTH, D_MODEL)),
        "w_up": nrm(ks[18], (DEPTH, D_MODEL, 2 * D_FF), D_MODEL ** -0.5),
        "ffn_conv_w": nrm(ks[19], (DEPTH, FFN_CONV, 2 * D_FF), FFN_CONV ** -0.5),
        "ffn_conv_b": nrm(ks[20], (DEPTH, 2 * D_FF), 0.01),
        "w_down": nrm(ks[21], (DEPTH, D_FF, D_MODEL), D_FF ** -0.5),
        "final_norm_g": gain(ks[22], (D_MODEL,)),
    }


def reference(x, ln_mix_g, w_in, lru_conv_w, lru_conv_b, lru_gate_a_w, lru_gate_a_b,
              lru_gate_x_w, lru_gate_x_b, lru_lambda, q_norm_g, w_uq, kv_norm_g, w_ukv,
              grp_norm_lru_g, grp_norm_mla_g, w_o, ln_ffn_g, w_up, ffn_conv_w, ffn_conv_b,
              w_down, final_norm_g):
    cos, sin = rope_tables(x.shape[1], x.dtype)
    h = x
    for l in range(DEPTH):
        h = h + hybrid_mixer(rms_norm(h, ln_mix_g[l]), w_in[l], lru_conv_w[l], lru_conv_b[l],
                             lru_gate_a_w[l], lru_gate_a_b[l], lru_gate_x_w[l], lru_gate_x_b[l],
                             lru_lambda[l], q_norm_g[l], w_uq[l], kv_norm_g[l], w_ukv[l],
                             grp_norm_lru_g[l], grp_norm_mla_g[l], w_o[l], cos, sin)
        h = h + conv_glu_ffn(rms_norm(h, ln_ffn_g[l]), w_up[l], ffn_conv_w[l], ffn_conv_b[l], w_down[l])
    return rms_norm(h, final_norm_g)
```

```python
import math
import numpy as np
import concourse.bass as bass
import concourse.mybir as mybir
from concourse.bass_utils import run_bass_kernel_spmd

F32 = mybir.dt.float32
BF16 = mybir.dt.bfloat16
ALU = mybir.AluOpType
AF = mybir.ActivationFunctionType

D = 1024
T = 2048
NSEQ = 2
NB = T // 128
NTILE = T // 512
INC = 1440
DFF = 2816
NJ = DFF // 128
EPS = 1e-6
QSCALE = 1.0 / math.sqrt(96.0)
GC1 = 2.0 * math.sqrt(2.0 / math.pi)
GC2 = GC1 * 0.044715

_o = 0
def _c(n):
    global _o
    s = _o
    _o += n
    return s
C_CW = _c(16)
C_CB = _c(4)
C_GBA = _c(8)
C_GBX = _c(8)
C_LAM = _c(8)
C_FW = _c(132)
C_FB = _c(44)
C_GMIX = _c(8)
C_GQ = _c(2)
C_GKV = _c(1)
C_GGRP = _c(8)
C_ROPE = _c(512)
C_ROPEQ = _c(512)
C_IDENT = _c(128)
C_GQKVB = _c(384)
C_GMLAB = _c(512)
NCST_SB = _o
C_FG = _c(1024)
C_GFFN = _c(1024)
C_GMIXB = _c(1024)
NCST = _o

_DTSIZE = {F32: 4, BF16: 2}


def _prod(xs):
    r = 1
    for v in xs:
        r *= int(v)
    return r


class _Op:
    __slots__ = ("eng", "emit", "deps", "id", "idx", "sig", "val", "dma", "dsem", "dval", "raw_same")


def _intervals(f0, dims, es):
    lo = f0
    hi = f0
    for st, n in dims:
        if st >= 0:
            hi += st * (n - 1)
        else:
            lo += st * (n - 1)
    hi += 1
    if len(dims) >= 2:
        st0, n0 = dims[0]
        inner_lo = 0
        inner_hi = 0
        for st, n in dims[1:]:
            if st >= 0:
                inner_hi += st * (n - 1)
            else:
                inner_lo += st * (n - 1)
        inner_hi += 1
        ext = inner_hi - inner_lo
        if 1 < n0 <= 64 and abs(st0) > 2 * ext:
            out = []
            for i in range(n0):
                out.extend(_intervals(f0 + st0 * i, dims[1:], es))
            return out
    return [(lo * es, hi * es)]


def region(ap):
    t = ap.tensor
    es = _DTSIZE[ap.dtype]
    dims = [(int(a), int(b)) for a, b in ap.ap]
    shp = [int(s) for s in t.shape]
    name = t.name
    if name in ("sb", "ps"):
        row = _prod(shp[1:])
        off = int(ap.offset)
        p0 = off // row
        f0 = off % row
        pn = dims[0][1]
        if name == "ps":
            q0 = p0 // 32 * 32
            q1 = (p0 + pn + 31) // 32 * 32
            return [(name, q0, q1, lo // 2048 * 2048, (hi + 2047) // 2048 * 2048)
                    for lo, hi in _intervals(f0, dims[1:], es)]
        return [(name, p0, p0 + pn, lo, hi) for lo, hi in _intervals(f0, dims[1:], es)]
    return [(name, 0, 1) + _bound(int(ap.offset), dims, es)]


def _bound(f0, dims, es):
    lo = f0
    hi = f0
    for st, n in dims:
        if st >= 0:
            hi += st * (n - 1)
        else:
            lo += st * (n - 1)
    return (lo * es, (hi + 1) * es)


class Prog:
    CE = ("pe", "act", "dve", "pool")

    def __init__(self):
        self.ops = []
        self.by_eng = {e: [] for e in self.CE + ("sp",)}
        self.recs = {}
        self.ndsem = 32
        self.dcount = [0] * self.ndsem
        self.dlast = [None] * self.ndsem
        self.dnext = {"sp": 0, "pool": 0}
        self.stopped = False

    def _scan(self, space, p0, p1, lo, hi, kinds):
        out = []
        for r in self.recs.get(space, ()):
            if r[4] in kinds and r[0] < p1 and p0 < r[1] and r[2] < hi and lo < r[3]:
                out.append(r)
        return out

    def add(self, eng, emit, reads=(), writes=(), track_dram=("out",), dma=False):
        if self.stopped:
            return None
        op = _Op()
        op.eng = eng
        op.emit = emit
        op.id = len(self.ops)
        op.deps = {}
        op.sig = False
        op.dma = (eng == "sp") or dma
        op.raw_same = set()
        self.ops.append(op)
        rregs = []
        wregs = []
        for ap in reads:
            for rg in region(ap):
                if rg[0] in ("sb", "ps") or rg[0] in track_dram:
                    rregs.append(rg)
        for ap in writes:
            for rg in region(ap):
                if rg[0] in ("sb", "ps") or rg[0] in track_dram:
                    wregs.append(rg)
        for (sp, p0, p1, lo, hi) in rregs:
            for r in self._scan(sp, p0, p1, lo, hi, "w"):
                op.deps[r[5]] = "raw"
            if sp == "ps":
                for r in self._scan(sp, p0, p1, lo, hi, "r"):
                    if self.ops[r[5]].eng != eng and r[5] not in op.deps:
                        op.deps[r[5]] = "rar"
        for (sp, p0, p1, lo, hi) in wregs:
            for r in self._scan(sp, p0, p1, lo, hi, "wr"):
                if r[5] not in op.deps:
                    op.deps[r[5]] = "war" if r[4] == "r" else "waw"
        for (sp, p0, p1, lo, hi) in wregs:
            lst = self.recs.setdefault(sp, [])
            lst[:] = [r for r in lst if not (p0 <= r[0] and r[1] <= p1 and lo <= r[2] and r[3] <= hi)]
            lst.append([p0, p1, lo, hi, "w", op.id])
        for (sp, p0, p1, lo, hi) in rregs:
            lst = self.recs.setdefault(sp, [])
            lst[:] = [r for r in lst if not (r[4] == "r" and self.ops[r[5]].eng == eng and not self.ops[r[5]].dma
                                             and p0 <= r[0] and r[1] <= p1 and lo <= r[2] and r[3] <= hi)]
            lst.append([p0, p1, lo, hi, "r", op.id])
        op.deps.pop(op.id, None)
        if op.dma:
            half = self.ndsem // 2
            k = self.dnext[eng] + (0 if eng == "sp" else half)
            self.dnext[eng] = (self.dnext[eng] + 1) % half
            op.dsem = k
            self.dcount[k] += 1
            op.dval = 16 * self.dcount[k]
            if self.dlast[k] is not None:
                op.deps.setdefault(self.dlast[k], "sem")
            self.dlast[k] = op.id
        self.by_eng[eng].append(op)
        return op

    def finalize(self):
        ops = self.ops
        for op in ops:
            need = {}
            for d, kind in op.deps.items():
                dop = ops[d]
                if dop.eng == op.eng and not dop.dma:
                    if op.eng == "pe":
                        continue
                need[d] = kind
            op.deps = need
            for d in need:
                ops[d].sig = True
        cnt = {e: 0 for e in self.CE}
        for op in ops:
            if not op.dma and op.sig:
                cnt[op.eng] += 1
                op.val = cnt[op.eng]
        clk = {e: {"c": {x: 0 for x in self.CE}, "d": [0] * self.ndsem} for e in self.CE + ("sp",)}
        snap = {}
        for op in ops:
            ck = clk[op.eng]
            waits = []
            for d in sorted(op.deps):
                dop = ops[d]
                if dop.dma:
                    if ck["d"][dop.dsem] < dop.dval:
                        waits.append(("d", dop.dsem, dop.dval))
                        ck["d"][dop.dsem] = dop.dval
                        self._merge(ck, snap[d])
                else:
                    if ck["c"][dop.eng] < dop.val:
                        waits.append(("c", dop.eng, dop.val))
                        ck["c"][dop.eng] = dop.val
                        self._merge(ck, snap[d])
            best = {}
            for k, s, v in waits:
                if (k, s) not in best or best[(k, s)] < v:
                    best[(k, s)] = v
            op.raw_same = best
            snap[op.id] = {"c": dict(ck["c"]), "d": list(ck["d"])}

    @staticmethod
    def _merge(ck, sn):
        for e, v in sn["c"].items():
            if ck["c"][e] < v:
                ck["c"][e] = v
        dd = ck["d"]
        sd = sn["d"]
        for i in range(len(dd)):
            if dd[i] < sd[i]:
                dd[i] = sd[i]

    def emit_engine(self, ename, eng, csems, dsems):
        for op in self.by_eng[ename]:
            for (k, s), v in op.raw_same.items():
                if k == "c":
                    eng.wait_ge(csems[s], v)
                else:
                    eng.wait_ge(dsems[s], v)
            ins = op.emit(eng)
            if op.dma:
                ins.then_inc(dsems[op.dsem], 16)
            elif op.sig:
                ins.then_inc(csems[op.eng], 1)
        if ename == "sp":
            for k in range(self.ndsem):
                if self.dcount[k]:
                    eng.wait_ge(dsems[k], 16 * self.dcount[k])


class Arena:
    def __init__(self, nbytes):
        self.free = [(0, nbytes)]
        self.live = {}

    def alloc(self, name, nbytes, top=False):
        nbytes = (nbytes + 63) // 64 * 64
        if top:
            for i in range(len(self.free) - 1, -1, -1):
                o, n = self.free[i]
                if n >= nbytes:
                    if n == nbytes:
                        self.free.pop(i)
                    else:
                        self.free[i] = (o, n - nbytes)
                    self.live[name] = (o + n - nbytes, nbytes)
                    return o + n - nbytes
            raise RuntimeError(f"arena OOM(top) for {name} ({nbytes}); free={self.free} live={self.live}")
        for i, (o, n) in enumerate(self.free):
            if n >= nbytes:
                if n == nbytes:
                    self.free.pop(i)
                else:
                    self.free[i] = (o + nbytes, n - nbytes)
                self.live[name] = (o, nbytes)
                return o
        raise RuntimeError(f"arena OOM for {name} ({nbytes}); free={self.free} live={self.live}")

    def release(self, name):
        o, n = self.live.pop(name)
        self.free.append((o, n))
        self.free.sort()
        m = []
        for o, n in self.free:
            if m and m[-1][0] + m[-1][1] == o:
                m[-1] = (m[-1][0], m[-1][1] + n)
            else:
                m.append((o, n))
        self.free = m


SB_WORDS = 52992
LIMIT = [None]
CAST_ENG = "dve"
import os
SKIP_KST = bool(os.environ.get("SKIP_KST"))
DUMP = []
DUMP_LAYOUT = {}
NDBG = 24576


class _Stop(Exception):
    pass


def build_program():
    nc = bass.Bass("TRN2", target_bir_lowering=False)
    x = nc.dram_tensor("x", [NSEQ, T, D], F32, kind="ExternalInput").ap()
    cst_d = nc.dram_tensor("cst", [128, NCST], F32, kind="ExternalInput").ap()
    w_in_d = nc.dram_tensor("w_in", [D, INC], F32, kind="ExternalInput").ap()
    gates_d = nc.dram_tensor("gates", [128, 2048], F32, kind="ExternalInput").ap()
    w_uq_d = nc.dram_tensor("w_uq", [256, 768], F32, kind="ExternalInput").ap()
    w_ukv_d = nc.dram_tensor("w_ukv", [128, 1024], F32, kind="ExternalInput").ap()
    w_o_d = nc.dram_tensor("w_o", [D, D], F32, kind="ExternalInput").ap()
    w_up_d = nc.dram_tensor("w_up", [D, 2 * DFF], F32, kind="ExternalInput").ap()
    w_dn_d = nc.dram_tensor("w_down", [DFF, D], F32, kind="ExternalInput").ap()
    out = nc.dram_tensor("out", [NSEQ, T, D], F32, kind="ExternalOutput").ap()
    dbg = nc.dram_tensor("dbg", [128, NDBG], F32, kind="ExternalOutput").ap() if LIMIT[0] else None

    P = Prog()
    A = Arena(SB_WORDS * 4)

    with (
        nc.sbuf_tensor("sb", [128, SB_WORDS], F32) as SB,
        nc.psum_tensor("ps", [128, 8, 512], F32) as PS,
    ):
        def view(off, dt, shape):
            n = _prod(shape)
            es = _DTSIZE[dt]
            assert off % 4 == 0 and (n * es) % 4 == 0
            ap = SB[:, off // 4:(off + n * es) // 4]
            if dt != F32:
                ap = ap.bitcast(dt)
            if len(shape) == 2:
                ap = ap.rearrange("p (a b) -> p a b", a=shape[0])
            elif len(shape) == 3:
                ap = ap.rearrange("p (a b c) -> p a b c", a=shape[0], b=shape[1])
            return ap

        def alloc(name, dt, shape, top=False):
            off = A.alloc(name, _prod(shape) * _DTSIZE[dt], top)
            return view(off, dt, shape)

        def psb(b):
            return PS[:, b, :]

        def psb_bf(b):
            return PS[:, b, :].bitcast(BF16)

        def dma(o, i):
            P.add("sp", lambda e: e.dma_start(out=o, in_=i), reads=[i], writes=[o])

        def cdma(o, i):
            P.add("pool", lambda e: e.dma_start(out=o, in_=i), reads=[i], writes=[o], dma=True)

        def act(o, i, func, bias=None, scale=None, accum=None, extra_reads=()):
            kw = {}
            rd = [i] + list(extra_reads)
            if bias is not None:
                kw["bias"] = bias
                if not isinstance(bias, float):
                    rd.append(bias)
            if scale is not None:
                kw["scale"] = scale
                if not isinstance(scale, float):
                    rd.append(scale)
            wr = [o]
            if accum is not None:
                kw["accum_out"] = accum
                wr.append(accum)
            P.add("act", lambda e: e.activation(out=o, in_=i, func=func, **kw), reads=rd, writes=wr)

        def tt(eng, o, a, b, op):
            P.add(eng, lambda e: e.tensor_tensor(out=o, in0=a, in1=b, op=op), reads=[a, b], writes=[o])

        def ts(eng, o, a, s1, s2, op0, op1=None):
            rd = [a] + [s for s in (s1, s2) if s is not None and not isinstance(s, float)]
            if op1 is None:
                P.add(eng, lambda e: e.tensor_scalar(out=o, in0=a, scalar1=s1, scalar2=None, op0=op0), reads=rd, writes=[o])
            else:
                P.add(eng, lambda e: e.tensor_scalar(out=o, in0=a, scalar1=s1, scalar2=s2, op0=op0, op1=op1), reads=rd, writes=[o])

        def stt(o, a, s, b, op0, op1):
            rd = [a, b] + ([] if isinstance(s, float) else [s])
            P.add("dve", lambda e: e.scalar_tensor_tensor(out=o, in0=a, scalar=s, in1=b, op0=op0, op1=op1), reads=rd, writes=[o])

        def cp(eng, o, i):
            if eng == "act":
                P.add("act", lambda e: e.copy(out=o, in_=i), reads=[i], writes=[o])
            else:
                P.add(eng, lambda e: e.tensor_copy(out=o, in_=i), reads=[i], writes=[o])

        def scast(eng, o, i, sc):
            if eng == "act":
                act(o, i, AF.Copy, scale=sc)
            else:
                ts(eng, o, i, sc, None, ALU.mult)

        def recip(o, i):
            P.add("dve", lambda e: e.reciprocal(out=o, in_=i), reads=[i], writes=[o])

        def mm(o, l, r, start, stop):
            P.add("pe", lambda e: e.matmul(o, l, r, start=start, stop=stop), reads=[l, r], writes=[o])

        def tr(o, i, ident):
            P.add("pe", lambda e: e.transpose(o, i, ident), reads=[i, ident], writes=[o])

        def memset(eng, o, v):
            P.add(eng, lambda e: e.memset(o, v), reads=[], writes=[o])

        def scan(o, d0, d1, init=0.0):
            rd = [d0, d1] + ([] if isinstance(init, float) else [init])
            P.add("dve", lambda e: e.tensor_tensor_scan(out=o, data0=d0, data1=d1, initial=init, op0=ALU.mult, op1=ALU.add),
                  reads=rd, writes=[o])

        def rstd_from_ss(o, ss, tmp, inv_n):
            n = int(tmp.shape[-1])
            act(tmp, ss, AF.Sqrt, bias=EPSC, scale=inv_n)
            recip(o, tmp)

        CST = alloc("cst", F32, [NCST_SB])
        BC = alloc("bc", F32, [D])
        ident = alloc("ident", BF16, [128])
        ones_bf = alloc("ones", BF16, [2])
        small = alloc("small", F32, [64])
        COEFH = small[:, 0:8]
        COEF1 = small[:, 8:16]
        GBAH = small[:, 16:24]
        GBXH = small[:, 24:32]
        QTR = small[:, 32:33]
        EPSC = small[:, 33:34]
        stats = alloc("stats", F32, [96])
        ig2 = alloc("ig2", BF16, [8])

        dma(CST, cst_d[:, 0:NCST_SB])
        cp("dve", ident, CST[:, C_IDENT:C_IDENT + 128])
        memset("pool", ones_bf, 1.0)
        memset("pool", QTR, 0.25)
        memset("pool", EPSC, EPS)
        tmp8 = small[:, 40:48]
        act(tmp8, CST[:, C_LAM:C_LAM + 8], AF.Exp, scale=-1.0)
        ts("dve", tmp8, tmp8, 1.0, None, ALU.add)
        act(tmp8, tmp8, AF.Ln)
        ts("dve", COEFH, tmp8, -4.0, None, ALU.mult)
        ts("dve", COEF1, tmp8, -8.0, None, ALU.mult)
        ts("dve", GBAH, CST[:, C_GBA:C_GBA + 8], 0.5, None, ALU.mult)
        ts("dve", GBXH, CST[:, C_GBX:C_GBX + 8], 0.5, None, ALU.mult)
        tmpg = small[:, 48:56]
        tt("dve", tmpg, CST[:, C_GGRP:C_GGRP + 8], CST[:, C_GGRP:C_GGRP + 8], ALU.mult)
        recip(tmpg, tmpg)
        cp("dve", ig2, tmpg)

        bank_rr = [0]

        def next_bank(n=1):
            b = bank_rr[0]
            if b + n > 8:
                b = 0
            bank_rr[0] = (b + n) % 8
            return b

        def next_bank4():
            b = 0 if bank_rr[0] in (0, 5, 6, 7) else 4
            bank_rr[0] = (b + 4) % 8
            return b

        def _chk(name):
            if LIMIT[0] == name and not P.stopped:
                pos = 0
                for nm in DUMP:
                    o, nbytes = A.live[nm]
                    w = nbytes // 4
                    dma(dbg[:, pos:pos + w], SB[:, o // 4:o // 4 + w])
                    DUMP_LAYOUT[nm] = (pos, w)
                    pos += w
                P.stopped = True

        def load_w_in(wb_=None):
            if wb_ is None:
                wb_ = alloc("w_in_b", BF16, [8, INC], top=True)
            for c in range(8):
                cdma(wb_[:, c, :], w_in_d[c * 128:(c + 1) * 128, :])
            return wb_

        w_in_next = [None]

        for s in range(NSEQ):
            yTl = alloc("yTl", BF16, [4, T], top=True)
            if s == 0:
                w_in_b = load_w_in()
            else:
                w_in_b = w_in_next[0]
            w_uq_b = alloc("w_uq_b", BF16, [2, 768], top=True)
            w_ukv_b = alloc("w_ukv_b", BF16, [8, 128], top=True)
            gate_b = alloc("gate_b", BF16, [16, 128], top=True)
            w_o_b = alloc("w_o_b", BF16, [8, D], top=True)

            def load_w_rest():
                cdma(gate_b.rearrange("p a b -> p (a b)"), gates_d[:, :])
                for c in range(2):
                    cdma(w_uq_b[:, c, :], w_uq_d[c * 128:(c + 1) * 128, :])
                cdma(w_ukv_b.rearrange("p a b -> p (a b)"), w_ukv_d[:, :])
                for c in range(8):
                    cdma(w_o_b[:, c, :], w_o_d[c * 128:(c + 1) * 128, :])

            _chk("W")
            xl = [alloc(f"xl{i}", F32, [T + 4]) for i in range(4)]
            gg = alloc("gg", BF16, [4, T])
            cqT = alloc("cqT", BF16, [2, T], top=True)
            ckvT = alloc("ckvT", BF16, [T], top=True)
            kpeT = alloc("kpeT", BF16, [T], top=True)
            xt = [alloc(f"xt{i}", F32, [D]) for i in range(4)]
            xn = alloc("xn", BF16, [4, D])
            xnT = [alloc(f"xnT{i}", BF16, [8, 512]) for i in range(2)]
            lat = alloc("lat", F32, [4, 416])
            clat = alloc("clat", BF16, [4, 384])
            kst = alloc("kst", BF16, [4, 96])
            gt = [alloc(f"gt{i}", F32, [512]) for i in range(4)]
            rt = alloc("rt", F32, [2, 64])

            for i in range(4):
                memset("pool", xl[i][:, 0:2], 0.0)
                memset("pool", xl[i][:, T + 2:T + 4], 0.0)
            memset("pool", kst[:, :, 0:64], 0.0)
            SS = stats[:, 0:4]
            RS = stats[:, 4:8]
            TM = stats[:, 8:16]
            SSQ = stats[:, 16:24]
            RSQ = stats[:, 24:32]
            def m1_xprep_ab(j):
                for b in range(4):
                    blk = j * 4 + b
                    dma(xt[b], x[s, blk * 128:(blk + 1) * 128, :])
                    act(xn[:, b, :], xt[b], AF.Square, accum=SS[:, b:b + 1])
                rstd_from_ss(RS[:, 0:4], SS[:, 0:4], stats[:, 80:84], 1.0 / D)
                for b in range(4):
                    stt(xn[:, b, :], xt[b], RS[:, b:b + 1], BC, ALU.mult, ALU.mult)

            def m1_xprep_t(j):
                xT = xnT[j % 2]
                for b in range(4):
                    bk = next_bank()
                    pb = psb_bf(bk)
                    for c in range(8):
                        tr(pb[:, c * 128:(c + 1) * 128], xn[:, b, c * 128:(c + 1) * 128], ident)
                    cp("act" if b % 2 else "dve", xT[:, :, b * 128:(b + 1) * 128],
                       pb.rearrange("p (c t) -> p c t", c=8))

            def m1_fm(j):
                xT = xnT[j % 2]
                for oc in range(8):
                    bk = next_bank()
                    pf = psb(bk)
                    for k in range(8):
                        mm(pf, w_in_b[:, k, oc * 128:(oc + 1) * 128], xT[:, k, :], k == 0, k == 7)
                    if oc < 4:
                        cp("act" if oc % 2 else "dve", xl[oc][:, 2 + j * 512:2 + (j + 1) * 512], pf)
                    else:
                        g0 = gt[(oc % 2) * 2]
                        g1 = gt[(oc % 2) * 2 + 1]
                        act(g0, pf, AF.Square)
                        stt(g0, g0, GC1 / GC2, pf, ALU.add, ALU.mult)
                        act(g1, g0, AF.Sigmoid, scale=GC2)
                        tt("dve", gg[:, oc - 4, j * 512:(j + 1) * 512], g1, pf, ALU.mult)

            def m1_tok_head(j):
                xT = xnT[j % 2]
                for b in range(4):
                    bk = next_bank()
                    pt = psb(bk)[:, 0:416]
                    for k in range(8):
                        mm(pt, xT[:, k, b * 128:(b + 1) * 128], w_in_b[:, k, 1024:1440], k == 0, k == 7)
                    cp("act", lat[:, b, :], pt)
                    act(clat[:, b, 0:256], lat[:, b, 0:256], AF.Square, accum=SSQ[:, b:b + 1])
                    act(clat[:, b, 256:384], lat[:, b, 256:384], AF.Square, accum=SSQ[:, 4 + b:5 + b])
                act(TM[:, 0:4], SSQ[:, 0:4], AF.Sqrt, bias=EPSC, scale=1.0 / 256)
                act(TM[:, 4:8], SSQ[:, 4:8], AF.Sqrt, bias=EPSC, scale=1.0 / 128)
                recip(RSQ[:, 0:8], TM[:, 0:8])
                for b in range(4):
                    blk = j * 4 + b
                    stt(clat[:, b, 0:256], lat[:, b, 0:256], RSQ[:, b:b + 1], CST[:, C_GQKVB:C_GQKVB + 256], ALU.mult, ALU.mult)
                    stt(clat[:, b, 256:384], lat[:, b, 256:384], RSQ[:, 4 + b:5 + b], CST[:, C_GQKVB + 256:C_GQKVB + 384], ALU.mult, ALU.mult)
                    kr = lat[:, b, 384:416].rearrange("p (a b) -> p a b", a=2)
                    cosb = CST[:, C_ROPE + blk * 32:C_ROPE + blk * 32 + 16].unsqueeze(1).to_broadcast([128, 2, 16])
                    sinb = CST[:, C_ROPE + blk * 32 + 16:C_ROPE + blk * 32 + 32].unsqueeze(1).to_broadcast([128, 2, 16])
                    t1 = rt[:, 0, 0:32].rearrange("p (a b) -> p a b", a=2)
                    t2 = rt[:, 1, 0:32].rearrange("p (a b) -> p a b", a=2)
                    tt("pool", t1, kr, cosb, ALU.mult)
                    tt("pool", t2, kr, sinb, ALU.mult)
                    tt("pool", kst[:, b, 64:80], t1[:, 0, :], t2[:, 1, :], ALU.subtract)
                    tt("pool", kst[:, b, 80:96], t1[:, 1, :], t2[:, 0, :], ALU.add)
                if j == 0:
                    load_w_rest()

            def m1_tok_tail(j):
                for b in range(4):
                    blk = j * 4 + b
                    bk = next_bank()
                    pb = psb_bf(bk)
                    tr(pb[:, 0:128], clat[:, b, 0:128], ident)
                    tr(pb[:, 128:256], clat[:, b, 128:256], ident)
                    tr(pb[:, 256:384], clat[:, b, 256:384], ident)
                    tr(pb[0:96, 384:512], kst[:, b, :], ident)
                    tok = slice(blk * 128, (blk + 1) * 128)
                    cp("dve", cqT[:, :, tok], pb[:, 0:256].rearrange("p (c t) -> p c t", c=2))
                    cp("dve", ckvT[:, tok], pb[:, 256:384])
                    cp("dve", kpeT[64:96, tok], pb[64:96, 384:512])

            dma(BC, cst_d[:, C_GMIXB:C_GMIXB + D])
            m1_xprep_ab(0)
            m1_xprep_t(0)
            for j in range(NTILE):
                if j + 1 < NTILE:
                    m1_xprep_ab(j + 1)
                m1_fm(j)
                if j >= 1:
                    m1_tok_tail(j - 1)
                if j + 1 < NTILE:
                    m1_xprep_t(j + 1)
                m1_tok_head(j)
            m1_tok_tail(NTILE - 1)
            for nm in ("xt0", "xt1", "xt2", "xt3", "xn", "xnT0", "xnT1", "lat", "clat", "kst", "gt0", "gt1", "gt2", "gt3", "rt", "w_in_b"):
                A.release(nm)

            _chk("M1")
            xcs = [alloc(f"xc{i}", F32, [T]) for i in range(2)]
            xcbs = [alloc(f"xcb{i}", BF16, [T]) for i in range(2)]
            HT = T // 2
            sets = [(alloc(f"ra{i}", F32, [HT]), alloc(f"a2{i}", F32, [HT]), alloc(f"iu{i}", F32, [HT])) for i in range(3)]
            lit = [0]
            h0 = alloc("h0", F32, [T])
            h1b = alloc("h1b", F32, [T])
            hs = [h0, h1b]

            def lru_conv(c):
                xc_ = xcs[c % 2]
                cw = lambda k: CST[:, C_CW + c * 4 + k:C_CW + c * 4 + k + 1]
                act(xc_, xl[c][:, 0:T], AF.Identity, bias=CST[:, C_CB + c:C_CB + c + 1], scale=cw(0))
                for k in range(1, 4):
                    stt(xc_, xl[c][:, k:k + T], cw(k), xc_, ALU.mult, ALU.add)
                cp("dve", xcbs[c % 2], xc_)
                A.release(f"xl{c}")

            lru_conv(0)
            for c in range(4):
                xc = xcs[c % 2]
                xcb = xcbs[c % 2]
                if c + 1 < 4:
                    lru_conv(c + 1)
                if c == 0:
                    sets.append((alloc("ra3", F32, [HT]), alloc("a23", F32, [HT]), alloc("iu3", F32, [HT])))
                for pair in (((0, 0), (1, 1)), ((0, 1), (1, 0))):
                    psets = [sets[(lit[0] + q) % len(sets)] for q in range(2)]
                    lit[0] += 2
                    for it, (d, hf) in enumerate(pair):
                        base = 4 * it
                        for gi in range(2):
                            for t in range(2):
                                mm(psb(base + gi * 2 + t), gate_b[:, c * 4 + d * 2 + gi, :],
                                   xcb[:, hf * HT + t * 512:hf * HT + (t + 1) * 512], True, True)
                    for it, (d, hf) in enumerate(pair):
                        ra_, a2_, iu_ = psets[it]
                        col = d * 4 + c
                        base = 4 * it
                        act(ra_.rearrange("p (a b) -> p a b", a=2), PS[:, base:base + 2, :], AF.Tanh, bias=GBAH[:, col:col + 1], scale=0.5)
                        act(iu_.rearrange("p (a b) -> p a b", a=2), PS[:, base + 2:base + 4, :], AF.Tanh, bias=GBXH[:, col:col + 1], scale=0.5)
                    for it, (d, hf) in enumerate(pair):
                        ra_, a2_, iu_ = psets[it]
                        col = d * 4 + c
                        act(a2_, ra_, AF.Exp, bias=COEF1[:, col:col + 1], scale=COEF1[:, col:col + 1])
                        act(ra_, ra_, AF.Exp, bias=COEFH[:, col:col + 1], scale=COEFH[:, col:col + 1])
                    for it, (d, hf) in enumerate(pair):
                        ra_, a2_, iu_ = psets[it]
                        act(a2_, a2_, AF.Sqrt, bias=QTR, scale=-0.25)
                    for it, (d, hf) in enumerate(pair):
                        ra_, a2_, iu_ = psets[it]
                        tsl = slice(hf * HT, (hf + 1) * HT)
                        stt(iu_, iu_, 1.0, xc[:, tsl], ALU.add, ALU.mult)
                        tt("dve", iu_, iu_, a2_, ALU.mult)
                        h = hs[d]
                        if d == 0:
                            init = 0.0 if hf == 0 else h[:, HT - 1:HT]
                            scan(h[:, tsl], ra_, iu_, init)
                        else:
                            init = 0.0 if hf == 1 else h[:, HT:HT + 1]
                            scan(h[:, tsl][:, ::-1], ra_[:, ::-1], iu_[:, ::-1], init)
                tt("dve", h0, h0, h1b, ALU.add)
                stt(yTl[:, c, :], h0, CST[:, C_GGRP + c:C_GGRP + c + 1], gg[:, c, :], ALU.mult, ALU.mult)
            for nm in ("xc0", "xc1", "xcb0", "xcb1", "ra0", "a20", "iu0", "ra1", "a21", "iu1", "ra2", "a22", "iu2", "ra3", "a23", "iu3", "h0", "h1b", "gg", "gate_b"):
                A.release(nm)

            _chk("M3")
            QT = alloc("QT", BF16, [8, T])
            KT = alloc("KT", BF16, [8, T])
            VA = alloc("VA", BF16, [NB * 8, 128])
            qsts = [alloc(f"qst{i}", BF16, [8, 96]) for i in range(2)]
            qt1s = [alloc(f"qt1{i}", F32, [8, 32]) for i in range(2)]
            qt2s = [alloc(f"qt2{i}", F32, [8, 32]) for i in range(2)]
            memset("dve", VA[:, :, 64:128], 1.0)
            for h in range(8):
                dma(KT[64:96, h, :], kpeT[64:96, :])

            def m2_head(blk):
                tok = slice(blk * 128, (blk + 1) * 128)
                qst, qt1, qt2 = qsts[blk % 2], qt1s[blk % 2], qt2s[blk % 2]
                bk = next_bank(2)
                for n in range(2):
                    for k in range(2):
                        mm(psb(bk + n)[:, 0:384], cqT[:, k, tok], w_uq_b[:, k, n * 384:(n + 1) * 384], k == 0, k == 1)
                cosq = CST[:, C_ROPEQ + blk * 32:C_ROPEQ + blk * 32 + 16]
                sinq = CST[:, C_ROPEQ + blk * 32 + 16:C_ROPEQ + blk * 32 + 32]
                for n in range(2):
                    q4 = psb(bk + n)[:, 0:384].rearrange("p (h d) -> p h d", h=4)
                    P.add("act", (lambda q4=q4, n=n, qst=qst: (lambda e: e.mul(out=qst[:, n * 4:(n + 1) * 4, 0:64], in_=q4[:, :, 0:64], mul=QSCALE)))(),
                          reads=[q4[:, :, 0:64]], writes=[qst[:, n * 4:(n + 1) * 4, 0:64]])
                    qpe = q4[:, :, 64:96].rearrange("p h (a b) -> p h a b", a=2)
                    cb_ = cosq.unsqueeze(1).unsqueeze(1).to_broadcast([128, 4, 2, 16])
                    sb_ = sinq.unsqueeze(1).unsqueeze(1).to_broadcast([128, 4, 2, 16])
                    tt("dve", qt1[:, n * 4:(n + 1) * 4, :].rearrange("p h (a b) -> p h a b", a=2), qpe, cb_, ALU.mult)
                    tt("dve", qt2[:, n * 4:(n + 1) * 4, :].rearrange("p h (a b) -> p h a b", a=2), qpe, sb_, ALU.mult)
                tt("pool", qst[:, :, 64:80], qt1[:, :, 0:16], qt2[:, :, 16:32], ALU.subtract)
                tt("pool", qst[:, :, 80:96], qt1[:, :, 16:32], qt2[:, :, 0:16], ALU.add)
                bk = next_bank()
                mm(psb(bk), ckvT[:, tok], w_ukv_b[:, :, 64:128], True, True)
                tt("dve", VA[:, blk * 8:(blk + 1) * 8, 0:64], psb(bk).rearrange("p (h d) -> p h d", h=8),
                   CST[:, C_GMLAB:C_GMLAB + 512].rearrange("p (h d) -> p h d", h=8), ALU.mult)

            def m2_tail(blk):
                tok = slice(blk * 128, (blk + 1) * 128)
                qst = qsts[blk % 2]
                bk = next_bank()
                pb = psb_bf(bk)
                for h in range(8):
                    tr(pb[0:96, h * 128:(h + 1) * 128], qst[:, h, :], ident)
                cp("act", QT[0:96, :, tok], pb[0:96, :].rearrange("p (h t) -> p h t", h=8))

            m2_head(0)
            for blk in range(NB):
                if blk + 1 < NB:
                    m2_head(blk + 1)
                m2_tail(blk)
            for t in range(NTILE):
                for h in range(8):
                    bk = next_bank()
                    mm(psb(bk)[0:64, :], w_ukv_b[:, h, 0:64], ckvT[:, t * 512:(t + 1) * 512], True, True)
                    cp("act" if h % 2 else "dve", KT[0:64, h, t * 512:(t + 1) * 512], psb(bk)[0:64, :])
            for nm in ("qst0", "qst1", "qt10", "qt11", "qt20", "qt21", "cqT", "ckvT", "kpeT", "w_uq_b", "w_ukv_b"):
                A.release(nm)

            _chk("M2")
            yTm = alloc("yTm", BF16, [4, T], top=True)
            PT = [alloc(f"PT{i}", BF16, [1024]) for i in range(4)]
            rden = alloc("rden", F32, [1024])
            it = 0
            for h in range(8):
                for qh in range(2):
                    obk = 4 + 2 * (it % 2)
                    q0 = qh * 1024

                    def s_mm(kb, sb0):
                        for t in range(2):
                            mm(psb(sb0 + t), KT[0:96, h, kb * 128:(kb + 1) * 128],
                               QT[0:96, h, q0 + t * 512:q0 + (t + 1) * 512], True, True)
                    s_mm(0, 0)
                    for kb in range(NB):
                        sb0 = 2 * (kb % 2)
                        if kb + 1 < NB:
                            s_mm(kb + 1, 2 * ((kb + 1) % 2))
                        pt = PT[kb % 4]
                        act(pt.rearrange("p (a b) -> p a b", a=2), PS[:, sb0:sb0 + 2, :], AF.Exp)
                        for t in range(2):
                            mm(psb(obk + t), VA[:, kb * 8 + h, :], pt[:, t * 512:(t + 1) * 512], kb == 0, kb == NB - 1)
                    ob = PS[:, obk:obk + 2, :]
                    recip(rden[0:64, :].rearrange("p (a b) -> p a b", a=2), ob[64:128, :, :])
                    po = (h % 2) * 64
                    tt("dve", yTm[po:po + 64, h // 2, q0:q0 + 1024].rearrange("p (a b) -> p a b", a=2),
                       ob[0:64, :, :], rden[0:64, :].rearrange("p (a b) -> p a b", a=2), ALU.mult)
                    it += 1
            for nm in ("PT0", "PT1", "PT2", "PT3", "rden", "QT", "KT", "VA"):
                A.release(nm)

            _chk("M4")
            hnT = alloc("hnT", BF16, [8, T])
            NJA = 19
            wupb = [alloc(f"wupb{i}", BF16, [8, 256]) for i in range(2)]
            wdbA = alloc("wdbA", BF16, [NJA, D])
            w_up_v = w_up_d.rearrange("(c p) n -> p c n", p=128)

            def load_wup(j):
                wb = wupb[j % 2]
                for part in range(2):
                    col = part * DFF + j * 128
                    cdma(wb[:, :, part * 128:(part + 1) * 128], w_up_v[:, :, col:col + 128])

            load_wup(0)
            load_wup(1)
            for j in range(NJA):
                cdma(wdbA[:, j, :], w_dn_d[j * 128:(j + 1) * 128, :])
            sqs = [alloc(f"sq{i}", BF16, [8, 512]) for i in range(2)]
            gst = alloc("gst", F32, [NB * 2])
            grs = alloc("grs", F32, [NB * 2])
            xr = [alloc(f"xr{i}", F32, [D]) for i in range(2)]
            h1t = [alloc(f"h1t{i}", F32, [D]) for i in range(2)]
            hn = alloc("hn", BF16, [D])
            junk = alloc("junk", BF16, [D])
            sbk = next_bank()
            for jt in range(NTILE):
                sq4 = sqs[jt % 2]
                tok4 = slice(jt * 512, (jt + 1) * 512)
                tt("dve", sq4[:, 0:4, :], yTl[:, :, tok4], yTl[:, :, tok4], ALU.mult)
                tt("dve", sq4[:, 4:8, :], yTm[:, :, tok4], yTm[:, :, tok4], ALU.mult)
                for b4 in range(4):
                    blk = jt * 4 + b4
                    for g in range(2):
                        col = blk * 2 + g
                        for c in range(4):
                            mm(psb(sbk)[:, col:col + 1], sq4[:, g * 4 + c, b4 * 128:(b4 + 1) * 128],
                               ig2[:, g * 4 + c:g * 4 + c + 1], c == 0, c == 3)
            cp("dve", gst, psb(sbk)[:, 0:NB * 2])
            act(gst, gst, AF.Sqrt, bias=EPSC, scale=1.0 / 512)
            recip(grs, gst)
            H1S = stats[:, 32:48]
            H1R = stats[:, 48:64]
            H1T = stats[:, 64:80]
            def m5_mm_half(i):
                blk, n = divmod(i, 2)
                tok = slice(blk * 128, (blk + 1) * 128)
                if n == 0:
                    dma(xr[blk % 2], x[s, tok, :])
                gb = 2 * (i % 3)
                for g in range(2):
                    for c in range(4):
                        mm(psb(gb + g), (yTl if g == 0 else yTm)[:, c, tok], w_o_b[:, g * 4 + c, n * 512:(n + 1) * 512], c == 0, c == 3)

            def m5_stt_half(i):
                blk, n = divmod(i, 2)
                xb = xr[blk % 2]
                hb = h1t[blk % 2]
                gb = 2 * (i % 3)
                stt(hb[:, n * 512:(n + 1) * 512], psb(gb), grs[:, blk * 2:blk * 2 + 1],
                    xb[:, n * 512:(n + 1) * 512], ALU.mult, ALU.add)
                stt(hb[:, n * 512:(n + 1) * 512], psb(gb + 1), grs[:, blk * 2 + 1:blk * 2 + 2],
                    hb[:, n * 512:(n + 1) * 512], ALU.mult, ALU.add)

            def m5_b(blk):
                tok = slice(blk * 128, (blk + 1) * 128)
                hb = h1t[blk % 2]
                rstd_from_ss(H1R[:, blk:blk + 1], H1S[:, blk:blk + 1], H1T[:, blk:blk + 1], 1.0 / D)
                stt(hn, hb, H1R[:, blk:blk + 1], BC, ALU.mult, ALU.mult)
                pb = psb_bf(6 + blk % 2)
                for c in range(8):
                    tr(pb[:, c * 128:(c + 1) * 128], hn[:, c * 128:(c + 1) * 128], ident)
                cp("act", hnT[:, :, tok], pb.rearrange("p (c t) -> p c t", c=8))

            dma(BC, cst_d[:, C_GFFN:C_GFFN + D])
            m5_mm_half(0)
            m5_mm_half(1)
            for i in range(2 * NB):
                blk, n = divmod(i, 2)
                tok = slice(blk * 128, (blk + 1) * 128)
                if i + 2 < 2 * NB:
                    m5_mm_half(i + 2)
                m5_stt_half(i)
                if n == 1:
                    hb = h1t[blk % 2]
                    dma(out[s, tok, :], hb)
                    act(junk, hb, AF.Square, accum=H1S[:, blk:blk + 1])
                elif blk >= 1:
                    m5_b(blk - 1)
            m5_b(NB - 1)
            bank_rr[0] = 0
            for nm in ("sq0", "sq1", "gst", "grs", "xr0", "xr1", "h1t0", "h1t1", "hn", "junk", "yTl", "yTm", "w_o_b"):
                A.release(nm)

            _chk("M5")
            actT = alloc("actT", BF16, [NJ, T])
            accg = alloc("accg", F32, [T])
            accv = alloc("accv", F32, [T])
            sgl = alloc("sgl", F32, [T])
            for j in range(NJ):
                wb = wupb[j % 2]
                if 1 <= j and j + 1 < NJ:
                    load_wup(j + 1)
                for part in range(2):
                    b0 = part * 4
                    for t in range(4):
                        for k in range(8):
                            mm(psb(b0 + t), wb[:, k, part * 128:(part + 1) * 128], hnT[:, k, t * 512:(t + 1) * 512], k == 0, k == 7)
                    jj = part * NJ + j
                    fw = lambda k: CST[:, C_FW + jj * 3 + k:C_FW + jj * 3 + k + 1]
                    acc = accg if part == 0 else accv
                    u = PS[:, b0:b0 + 4, :].rearrange("p a b -> p (a b)")
                    act(acc.rearrange("p (a b) -> p a b", a=4), PS[:, b0:b0 + 4, :], AF.Identity,
                        bias=CST[:, C_FB + jj:C_FB + jj + 1], scale=fw(1))
                    stt(acc[:, 1:T], u[:, 0:T - 1], fw(0), acc[:, 1:T], ALU.mult, ALU.add)
                    stt(acc[:, 0:T - 1], u[:, 1:T], fw(2), acc[:, 0:T - 1], ALU.mult, ALU.add)
                act(sgl, accg, AF.Silu)
                tt("dve", actT[:, j, :], sgl, accv, ALU.mult)
            for nm in ("wupb0", "wupb1", "accg", "accv", "sgl", "hnT"):
                A.release(nm)

            if s + 1 < NSEQ:
                w_in_next[0] = alloc("w_in_b", BF16, [8, INC])
            dma(BC, cst_d[:, C_FG:C_FG + D])
            wdbB = alloc("wdbB", BF16, [NJ - NJA, D])
            for j in range(NJA, NJ):
                cdma(wdbB[:, j - NJA, :], w_dn_d[j * 128:(j + 1) * 128, :])
            hr = [alloc(f"hr{i}", F32, [D]) for i in range(2)]
            yo = [alloc(f"yo{i}", F32, [D]) for i in range(2)]
            junk = alloc("junk", BF16, [D])
            FS = stats[:, 32:48]
            FR = stats[:, 48:64]
            FT = stats[:, 64:80]
            for blk in range(NB):
                tok = slice(blk * 128, (blk + 1) * 128)
                hb = hr[blk % 2]
                yb = yo[blk % 2]
                dma(hb, out[s, tok, :])
                bk = next_bank(2)
                for n in range(2):
                    for j in range(NJ):
                        mm(psb(bk + n), actT[:, j, tok], (wdbA[:, j, n * 512:(n + 1) * 512] if j < NJA else wdbB[:, j - NJA, n * 512:(n + 1) * 512]), j == 0, j == NJ - 1)
                for n in range(2):
                    tt("dve", yb[:, n * 512:(n + 1) * 512], psb(bk + n), hb[:, n * 512:(n + 1) * 512], ALU.add)
                act(junk, yb, AF.Square, accum=FS[:, blk:blk + 1])
                rstd_from_ss(FR[:, blk:blk + 1], FS[:, blk:blk + 1], FT[:, blk:blk + 1], 1.0 / D)
                stt(yb, yb, FR[:, blk:blk + 1], BC, ALU.mult, ALU.mult)
                dma(out[s, tok, :], yb)
                if blk == 3 and s + 1 < NSEQ:
                    load_w_in(w_in_next[0])
            for nm in ("wdbA", "wdbB", "hr0", "hr1", "yo0", "yo1", "junk", "actT"):
                A.release(nm)

        P.finalize()
        import contextlib
        with contextlib.ExitStack() as es:
            csems = {e: es.enter_context(nc.semaphore(f"c_{e}")) for e in Prog.CE}
            dsems = [es.enter_context(nc.semaphore(f"d_{i}")) for i in range(P.ndsem)]
            block = es.enter_context(nc.Block())

            @block.sync
            def _(eng):
                P.emit_engine("sp", eng, csems, dsems)

            @block.tensor
            def _(eng):
                P.emit_engine("pe", eng, csems, dsems)

            @block.scalar
            def _(eng):
                P.emit_engine("act", eng, csems, dsems)

            @block.vector
            def _(eng):
                P.emit_engine("dve", eng, csems, dsems)

            @block.gpsimd
            def _(eng):
                P.emit_engine("pool", eng, csems, dsems)
    return nc


def _pc(v, nchunk):
    return np.ascontiguousarray(np.asarray(v, np.float32).reshape(nchunk, 128).T)


def _build_consts(inp):
    cst = np.zeros((128, NCST), np.float32)
    cw = np.asarray(inp["lru_conv_w"], np.float32)[0]
    cst[:, C_CW:C_CW + 16] = cw.reshape(4, 4, 128).transpose(2, 1, 0).reshape(128, 16)
    cst[:, C_CB:C_CB + 4] = _pc(np.asarray(inp["lru_conv_b"])[0], 4)
    for name, off in (("lru_gate_a_b", C_GBA), ("lru_gate_x_b", C_GBX), ("lru_lambda", C_LAM)):
        v = np.asarray(inp[name], np.float32)[0]
        cst[:, off:off + 8] = v.reshape(2, 4, 128).transpose(2, 0, 1).reshape(128, 8)
    fw = np.asarray(inp["ffn_conv_w"], np.float32)[0]
    cst[:, C_FW:C_FW + 132] = fw.reshape(3, 44, 128).transpose(2, 1, 0).reshape(128, 132)
    cst[:, C_FB:C_FB + 44] = _pc(np.asarray(inp["ffn_conv_b"])[0], 44)
    cst[:, C_GMIX:C_GMIX + 8] = _pc(np.asarray(inp["ln_mix_g"])[0], 8)
    cst[:, C_GQ:C_GQ + 2] = _pc(np.asarray(inp["q_norm_g"])[0], 2)
    cst[:, C_GKV:C_GKV + 1] = _pc(np.asarray(inp["kv_norm_g"])[0], 1)
    ggrp = np.concatenate([np.asarray(inp["grp_norm_lru_g"], np.float32)[0], np.asarray(inp["grp_norm_mla_g"], np.float32)[0]])
    cst[:, C_GGRP:C_GGRP + 8] = _pc(ggrp, 8)
    half = 16
    freqs = (np.float32(10000.0) ** (-(np.arange(half, dtype=np.float32) / np.float32(half)))).astype(np.float32)
    pos = np.arange(T, dtype=np.float32)
    ang = (pos[:, None] * freqs[None, :]).astype(np.float32)
    cs = np.concatenate([np.cos(ang), np.sin(ang)], axis=1).astype(np.float32)
    tab = cs.reshape(NB, 128, 32).transpose(1, 0, 2).reshape(128, NB * 32)
    cst[:, C_ROPE:C_ROPE + 512] = tab
    cst[:, C_ROPEQ:C_ROPEQ + 512] = tab * np.float32(QSCALE)
    cst[:, C_FG:C_FG + D] = np.broadcast_to(np.asarray(inp["final_norm_g"], np.float32)[None, :], (128, D))
    cst[:, C_GFFN:C_GFFN + D] = np.broadcast_to(np.asarray(inp["ln_ffn_g"], np.float32)[0][None, :], (128, D))
    cst[:, C_IDENT:C_IDENT + 128] = np.eye(128, dtype=np.float32)
    cst[:, C_GMIXB:C_GMIXB + D] = np.broadcast_to(np.asarray(inp["ln_mix_g"], np.float32)[0][None, :], (128, D))
    gqkv = np.concatenate([np.asarray(inp["q_norm_g"], np.float32)[0], np.asarray(inp["kv_norm_g"], np.float32)[0]])
    cst[:, C_GQKVB:C_GQKVB + 384] = np.broadcast_to(gqkv[None, :], (128, 384))
    cst[:, C_GMLAB:C_GMLAB + 512] = np.broadcast_to(np.asarray(inp["grp_norm_mla_g"], np.float32)[0][None, :], (128, 512))
    return cst


def _build_gates(inp):
    ga = np.asarray(inp["lru_gate_a_w"], np.float32)[0]
    gx = np.asarray(inp["lru_gate_x_w"], np.float32)[0]
    g = np.zeros((128, 4, 4, 128), np.float32)
    for c in range(4):
        for d in range(2):
            for gi, w in enumerate((ga, gx)):
                for hh in range(2):
                    g[hh * 64:(hh + 1) * 64, c, d * 2 + gi, hh * 64:(hh + 1) * 64] = w[d, 2 * c + hh]
    return g.reshape(128, 2048)


_NC_CACHE = {}


def kernel(**inputs):
    inp = {k: np.asarray(v) for k, v in inputs.items()}
    n = 8
    xfull = np.ascontiguousarray(inp["x"], dtype=np.float32)
    cst = _build_consts(inp)
    gates = _build_gates(inp)
    shared = {
        "cst": cst,
        "w_in": np.ascontiguousarray(inp["w_in"][0], np.float32),
        "gates": gates,
        "w_uq": np.ascontiguousarray(inp["w_uq"][0].reshape(256, 768), np.float32),
        "w_ukv": np.ascontiguousarray(inp["w_ukv"][0].reshape(128, 1024), np.float32),
        "w_o": np.ascontiguousarray(inp["w_o"][0], np.float32),
        "w_up": np.ascontiguousarray(inp["w_up"][0], np.float32),
        "w_down": np.ascontiguousarray(inp["w_down"][0], np.float32),
    }
    if "nc" not in _NC_CACHE:
        _NC_CACHE["nc"] = build_program()
    nc = _NC_CACHE["nc"]
    in_maps = []
    for c in range(n):
        m = dict(shared)
        m["x"] = np.ascontiguousarray(xfull[c * NSEQ:(c + 1) * NSEQ])
        in_maps.append(m)
    res = run_bass_kernel_spmd(nc, in_maps, core_ids=list(range(n)))
    outs = [np.asarray(r["out"], dtype=np.float32) for r in res.results]
    return np.concatenate(outs, axis=0)
```

```python
import math
import numpy as np
import concourse.bass as bass
import concourse.mybir as mybir
from concourse.bass_utils import run_bass_kernel_spmd

F32 = mybir.dt.float32
BF16 = mybir.dt.bfloat16
ALU = mybir.AluOpType
AF = mybir.ActivationFunctionType

D = 1024
T = 2048
NSEQ = 2
NB = T // 128
NTILE = T // 512
INC = 1440
DFF = 2816
NJ = DFF // 128
EPS = 1e-6
QSCALE = 1.0 / math.sqrt(96.0)
GC1 = 2.0 * math.sqrt(2.0 / math.pi)
GC2 = GC1 * 0.044715

_o = 0
def _c(n):
    global _o
    s = _o
    _o += n
    return s
C_CW = _c(16)
C_CB = _c(4)
C_GBA = _c(8)
C_GBX = _c(8)
C_LAM = _c(8)
C_FW = _c(132)
C_FB = _c(44)
C_GMIX = _c(8)
C_GQ = _c(2)
C_GKV = _c(1)
C_GGRP = _c(8)
C_ROPE = _c(512)
C_ROPEQ = _c(512)
C_IDENT = _c(128)
C_GQKVB = _c(384)
C_GMLAB = _c(512)
NCST_SB = _o
C_FG = _c(1024)
C_GFFN = _c(1024)
C_GMIXB = _c(1024)
NCST = _o

_DTSIZE = {F32: 4, BF16: 2}


def _prod(xs):
    r = 1
    for v in xs:
        r *= int(v)
    return r


class _Op:
    __slots__ = ("eng", "emit", "deps", "id", "idx", "sig", "val", "dma", "dsem", "dval", "raw_same")


def _intervals(f0, dims, es):
    lo = f0
    hi = f0
    for st, n in dims:
        if st >= 0:
            hi += st * (n - 1)
        else:
            lo += st * (n - 1)
    hi += 1
    if len(dims) >= 2:
        st0, n0 = dims[0]
        inner_lo = 0
        inner_hi = 0
        for st, n in dims[1:]:
            if st >= 0:
                inner_hi += st * (n - 1)
            else:
                inner_lo += st * (n - 1)
        inner_hi += 1
        ext = inner_hi - inner_lo
        if 1 < n0 <= 64 and abs(st0) > 2 * ext:
            out = []
            for i in range(n0):
                out.extend(_intervals(f0 + st0 * i, dims[1:], es))
            return out
    return [(lo * es, hi * es)]


def region(ap):
    t = ap.tensor
    es = _DTSIZE[ap.dtype]
    dims = [(int(a), int(b)) for a, b in ap.ap]
    shp = [int(s) for s in t.shape]
    name = t.name
    if name in ("sb", "ps"):
        row = _prod(shp[1:])
        off = int(ap.offset)
        p0 = off // row
        f0 = off % row
        pn = dims[0][1]
        if name == "ps":
            q0 = p0 // 32 * 32
            q1 = (p0 + pn + 31) // 32 * 32
            return [(name, q0, q1, lo // 2048 * 2048, (hi + 2047) // 2048 * 2048)
                    for lo, hi in _intervals(f0, dims[1:], es)]
        return [(name, p0, p0 + pn, lo, hi) for lo, hi in _intervals(f0, dims[1:], es)]
    return [(name, 0, 1) + _bound(int(ap.offset), dims, es)]


def _bound(f0, dims, es):
    lo = f0
    hi = f0
    for st, n in dims:
        if st >= 0:
            hi += st * (n - 1)
        else:
            lo += st * (n - 1)
    return (lo * es, (hi + 1) * es)


class Prog:
    CE = ("pe", "act", "dve", "pool")

    def __init__(self):
        self.ops = []
        self.by_eng = {e: [] for e in self.CE + ("sp",)}
        self.recs = {}
        self.ndsem = 32
        self.dcount = [0] * self.ndsem
        self.dlast = [None] * self.ndsem
        self.dnext = {"sp": 0, "pool": 0}
        self.stopped = False

    def _scan(self, space, p0, p1, lo, hi, kinds):
        out = []
        for r in self.recs.get(space, ()):
            if r[4] in kinds and r[0] < p1 and p0 < r[1] and r[2] < hi and lo < r[3]:
                out.append(r)
        return out

    def add(self, eng, emit, reads=(), writes=(), track_dram=("out",), dma=False):
        if self.stopped:
            return None
        op = _Op()
        op.eng = eng
        op.emit = emit
        op.id = len(self.ops)
        op.deps = {}
        op.sig = False
        op.dma = (eng == "sp") or dma
        op.raw_same = set()
        self.ops.append(op)
        rregs = []
        wregs = []
        for ap in reads:
            for rg in region(ap):
                if rg[0] in ("sb", "ps") or rg[0] in track_dram:
                    rregs.append(rg)
        for ap in writes:
            for rg in region(ap):
                if rg[0] in ("sb", "ps") or rg[0] in track_dram:
                    wregs.append(rg)
        for (sp, p0, p1, lo, hi) in rregs:
            for r in self._scan(sp, p0, p1, lo, hi, "w"):
                op.deps[r[5]] = "raw"
            if sp == "ps":
                for r in self._scan(sp, p0, p1, lo, hi, "r"):
                    if self.ops[r[5]].eng != eng and r[5] not in op.deps:
                        op.deps[r[5]] = "rar"
        for (sp, p0, p1, lo, hi) in wregs:
            for r in self._scan(sp, p0, p1, lo, hi, "wr"):
                if r[5] not in op.deps:
                    op.deps[r[5]] = "war" if r[4] == "r" else "waw"
        for (sp, p0, p1, lo, hi) in wregs:
            lst = self.recs.setdefault(sp, [])
            lst[:] = [r for r in lst if not (p0 <= r[0] and r[1] <= p1 and lo <= r[2] and r[3] <= hi)]
            lst.append([p0, p1, lo, hi, "w", op.id])
        for (sp, p0, p1, lo, hi) in rregs:
            lst = self.recs.setdefault(sp, [])
            lst[:] = [r for r in lst if not (r[4] == "r" and self.ops[r[5]].eng == eng and not self.ops[r[5]].dma
                                             and p0 <= r[0] and r[1] <= p1 and lo <= r[2] and r[3] <= hi)]
            lst.append([p0, p1, lo, hi, "r", op.id])
        op.deps.pop(op.id, None)
        if op.dma:
            half = self.ndsem // 2
            k = self.dnext[eng] + (0 if eng == "sp" else half)
            self.dnext[eng] = (self.dnext[eng] + 1) % half
            op.dsem = k
            self.dcount[k] += 1
            op.dval = 16 * self.dcount[k]
            if self.dlast[k] is not None:
                op.deps.setdefault(self.dlast[k], "sem")
            self.dlast[k] = op.id
        self.by_eng[eng].append(op)
        return op

    def finalize(self):
        ops = self.ops
        for op in ops:
            need = {}
            for d, kind in op.deps.items():
                dop = ops[d]
                if dop.eng == op.eng and not dop.dma:
                    if op.eng == "pe":
                        continue
                need[d] = kind
            op.deps = need
            for d in need:
                ops[d].sig = True
        cnt = {e: 0 for e in self.CE}
        for op in ops:
            if not op.dma and op.sig:
                cnt[op.eng] += 1
                op.val = cnt[op.eng]
        clk = {e: {"c": {x: 0 for x in self.CE}, "d": [0] * self.ndsem} for e in self.CE + ("sp",)}
        snap = {}
        for op in ops:
            ck = clk[op.eng]
            waits = []
            for d in sorted(op.deps):
                dop = ops[d]
                if dop.dma:
                    if ck["d"][dop.dsem] < dop.dval:
                        waits.append(("d", dop.dsem, dop.dval))
                        ck["d"][dop.dsem] = dop.dval
                        self._merge(ck, snap[d])
                else:
                    if ck["c"][dop.eng] < dop.val:
                        waits.append(("c", dop.eng, dop.val))
                        ck["c"][dop.eng] = dop.val
                        self._merge(ck, snap[d])
            best = {}
            for k, s, v in waits:
                if (k, s) not in best or best[(k, s)] < v:
                    best[(k, s)] = v
            op.raw_same = best
            snap[op.id] = {"c": dict(ck["c"]), "d": list(ck["d"])}

    @staticmethod
    def _merge(ck, sn):
        for e, v in sn["c"].items():
            if ck["c"][e] < v:
                ck["c"][e] = v
        dd = ck["d"]
        sd = sn["d"]
        for i in range(len(dd)):
            if dd[i] < sd[i]:
                dd[i] = sd[i]

    def emit_engine(self, ename, eng, csems, dsems):
        for op in self.by_eng[ename]:
            for (k, s), v in op.raw_same.items():
                if k == "c":
                    eng.wait_ge(csems[s], v)
                else:
                    eng.wait_ge(dsems[s], v)
            ins = op.emit(eng)
            if op.dma:
                ins.then_inc(dsems[op.dsem], 16)
            elif op.sig:
                ins.then_inc(csems[op.eng], 1)
        if ename == "sp":
            for k in range(self.ndsem):
                if self.dcount[k]:
                    eng.wait_ge(dsems[k], 16 * self.dcount[k])


class Arena:
    def __init__(self, nbytes):
        self.free = [(0, nbytes)]
        self.live = {}

    def alloc(self, name, nbytes, top=False):
        nbytes = (nbytes + 63) // 64 * 64
        if top:
            for i in range(len(self.free) - 1, -1, -1):
                o, n = self.free[i]
                if n >= nbytes:
                    if n == nbytes:
                        self.free.pop(i)
                    else:
                        self.free[i] = (o, n - nbytes)
                    self.live[name] = (o + n - nbytes, nbytes)
                    return o + n - nbytes
            raise RuntimeError(f"arena OOM(top) for {name} ({nbytes}); free={self.free} live={self.live}")
        for i, (o, n) in enumerate(self.free):
            if n >= nbytes:
                if n == nbytes:
                    self.free.pop(i)
                else:
                    self.free[i] = (o + nbytes, n - nbytes)
                self.live[name] = (o, nbytes)
                return o
        raise RuntimeError(f"arena OOM for {name} ({nbytes}); free={self.free} live={self.live}")

    def release(self, name):
        o, n = self.live.pop(name)
        self.free.append((o, n))
        self.free.sort()
        m = []
        for o, n in self.free:
            if m and m[-1][0] + m[-1][1] == o:
                m[-1] = (m[-1][0], m[-1][1] + n)
            else:
                m.append((o, n))
        self.free = m


SB_WORDS = 52992
LIMIT = [None]
CAST_ENG = "dve"
import os
SKIP_KST = bool(os.environ.get("SKIP_KST"))
DUMP = []
DUMP_LAYOUT = {}
NDBG = 24576


class _Stop(Exception):
    pass


def build_program():
    nc = bass.Bass("TRN2", target_bir_lowering=False)
    x = nc.dram_tensor("x", [NSEQ, T, D], F32, kind="ExternalInput").ap()
    cst_d = nc.dram_tensor("cst", [128, NCST], F32, kind="ExternalInput").ap()
    w_in_d = nc.dram_tensor("w_in", [D, INC], F32, kind="ExternalInput").ap()
    gates_d = nc.dram_tensor("gates", [128, 2048], F32, kind="ExternalInput").ap()
    w_uq_d = nc.dram_tensor("w_uq", [256, 768], F32, kind="ExternalInput").ap()
    w_ukv_d = nc.dram_tensor("w_ukv", [128, 1024], F32, kind="ExternalInput").ap()
    w_o_d = nc.dram_tensor("w_o", [D, D], F32, kind="ExternalInput").ap()
    w_up_d = nc.dram_tensor("w_up", [D, 2 * DFF], F32, kind="ExternalInput").ap()
    w_dn_d = nc.dram_tensor("w_down", [DFF, D], F32, kind="ExternalInput").ap()
    out = nc.dram_tensor("out", [NSEQ, T, D], F32, kind="ExternalOutput").ap()
    dbg = nc.dram_tensor("dbg", [128, NDBG], F32, kind="ExternalOutput").ap() if LIMIT[0] else None

    P = Prog()
    A = Arena(SB_WORDS * 4)

    with (
        nc.sbuf_tensor("sb", [128, SB_WORDS], F32) as SB,
        nc.psum_tensor("ps", [128, 8, 512], F32) as PS,
    ):
        def view(off, dt, shape):
            n = _prod(shape)
            es = _DTSIZE[dt]
            assert off % 4 == 0 and (n * es) % 4 == 0
            ap = SB[:, off // 4:(off + n * es) // 4]
            if dt != F32:
                ap = ap.bitcast(dt)
            if len(shape) == 2:
                ap = ap.rearrange("p (a b) -> p a b", a=shape[0])
            elif len(shape) == 3:
                ap = ap.rearrange("p (a b c) -> p a b c", a=shape[0], b=shape[1])
            return ap

        def alloc(name, dt, shape, top=False):
            off = A.alloc(name, _prod(shape) * _DTSIZE[dt], top)
            return view(off, dt, shape)

        def psb(b):
            return PS[:, b, :]

        def psb_bf(b):
            return PS[:, b, :].bitcast(BF16)

        def dma(o, i):
            P.add("sp", lambda e: e.dma_start(out=o, in_=i), reads=[i], writes=[o])

        def cdma(o, i):
            P.add("pool", lambda e: e.dma_start(out=o, in_=i), reads=[i], writes=[o], dma=True)

        def act(o, i, func, bias=None, scale=None, accum=None, extra_reads=()):
            kw = {}
            rd = [i] + list(extra_reads)
            if bias is not None:
                kw["bias"] = bias
                if not isinstance(bias, float):
                    rd.append(bias)
            if scale is not None:
                kw["scale"] = scale
                if not isinstance(scale, float):
                    rd.append(scale)
            wr = [o]
            if accum is not None:
                kw["accum_out"] = accum
                wr.append(accum)
            P.add("act", lambda e: e.activation(out=o, in_=i, func=func, **kw), reads=rd, writes=wr)

        def tt(eng, o, a, b, op):
            P.add(eng, lambda e: e.tensor_tensor(out=o, in0=a, in1=b, op=op), reads=[a, b], writes=[o])

        def ts(eng, o, a, s1, s2, op0, op1=None):
            rd = [a] + [s for s in (s1, s2) if s is not None and not isinstance(s, float)]
            if op1 is None:
                P.add(eng, lambda e: e.tensor_scalar(out=o, in0=a, scalar1=s1, scalar2=None, op0=op0), reads=rd, writes=[o])
            else:
                P.add(eng, lambda e: e.tensor_scalar(out=o, in0=a, scalar1=s1, scalar2=s2, op0=op0, op1=op1), reads=rd, writes=[o])

        def stt(o, a, s, b, op0, op1):
            rd = [a, b] + ([] if isinstance(s, float) else [s])
            P.add("dve", lambda e: e.scalar_tensor_tensor(out=o, in0=a, scalar=s, in1=b, op0=op0, op1=op1), reads=rd, writes=[o])

        def cp(eng, o, i):
            if eng == "act":
                P.add("act", lambda e: e.copy(out=o, in_=i), reads=[i], writes=[o])
            else:
                P.add(eng, lambda e: e.tensor_copy(out=o, in_=i), reads=[i], writes=[o])

        def scast(eng, o, i, sc):
            if eng == "act":
                act(o, i, AF.Copy, scale=sc)
            else:
                ts(eng, o, i, sc, None, ALU.mult)

        def recip(o, i):
            P.add("dve", lambda e: e.reciprocal(out=o, in_=i), reads=[i], writes=[o])

        def mm(o, l, r, start, stop):
            P.add("pe", lambda e: e.matmul(o, l, r, start=start, stop=stop), reads=[l, r], writes=[o])

        def tr(o, i, ident):
            P.add("pe", lambda e: e.transpose(o, i, ident), reads=[i, ident], writes=[o])

        def memset(eng, o, v):
            P.add(eng, lambda e: e.memset(o, v), reads=[], writes=[o])

        def scan(o, d0, d1, init=0.0):
            rd = [d0, d1] + ([] if isinstance(init, float) else [init])
            P.add("dve", lambda e: e.tensor_tensor_scan(out=o, data0=d0, data1=d1, initial=init, op0=ALU.mult, op1=ALU.add),
                  reads=rd, writes=[o])

        def rstd_from_ss(o, ss, tmp, inv_n):
            n = int(tmp.shape[-1])
            act(tmp, ss, AF.Sqrt, bias=EPSC, scale=inv_n)
            recip(o, tmp)

        CST = alloc("cst", F32, [NCST_SB])
        BC = alloc("bc", F32, [D])
        ident = alloc("ident", BF16, [128])
        ones_bf = alloc("ones", BF16, [2])
        small = alloc("small", F32, [64])
        COEFH = small[:, 0:8]
        COEF1 = small[:, 8:16]
        GBAH = small[:, 16:24]
        GBXH = small[:, 24:32]
        QTR = small[:, 32:33]
        EPSC = small[:, 33:34]
        stats = alloc("stats", F32, [96])
        ig2 = alloc("ig2", BF16, [8])

        dma(CST, cst_d[:, 0:NCST_SB])
        cp("dve", ident, CST[:, C_IDENT:C_IDENT + 128])
        memset("pool", ones_bf, 1.0)
        memset("pool", QTR, 0.25)
        memset("pool", EPSC, EPS)
        tmp8 = small[:, 40:48]
        act(tmp8, CST[:, C_LAM:C_LAM + 8], AF.Exp, scale=-1.0)
        ts("dve", tmp8, tmp8, 1.0, None, ALU.add)
        act(tmp8, tmp8, AF.Ln)
        ts("dve", COEFH, tmp8, -4.0, None, ALU.mult)
        ts("dve", COEF1, tmp8, -8.0, None, ALU.mult)
        ts("dve", GBAH, CST[:, C_GBA:C_GBA + 8], 0.5, None, ALU.mult)
        ts("dve", GBXH, CST[:, C_GBX:C_GBX + 8], 0.5, None, ALU.mult)
        tmpg = small[:, 48:56]
        tt("dve", tmpg, CST[:, C_GGRP:C_GGRP + 8], CST[:, C_GGRP:C_GGRP + 8], ALU.mult)
        recip(tmpg, tmpg)
        cp("dve", ig2, tmpg)

        bank_rr = [0]

        def next_bank(n=1):
            b = bank_rr[0]
            if b + n > 8:
                b = 0
            bank_rr[0] = (b + n) % 8
            return b

        def next_bank4():
            b = 0 if bank_rr[0] in (0, 5, 6, 7) else 4
            bank_rr[0] = (b + 4) % 8
            return b

        def _chk(name):
            if LIMIT[0] == name and not P.stopped:
                pos = 0
                for nm in DUMP:
                    o, nbytes = A.live[nm]
                    w = nbytes // 4
                    dma(dbg[:, pos:pos + w], SB[:, o // 4:o // 4 + w])
                    DUMP_LAYOUT[nm] = (pos, w)
                    pos += w
                P.stopped = True

        def load_w_in(wb_=None):
            if wb_ is None:
                wb_ = alloc("w_in_b", BF16, [8, INC], top=True)
            for lo_, hi_ in ((0, 512), (512, 1024), (1024, INC)):
                for c in range(8):
                    cdma(wb_[:, c, lo_:hi_], w_in_d[c * 128:(c + 1) * 128, lo_:hi_])
            return wb_

        w_in_next = [None]

        for s in range(NSEQ):
            yTl = alloc("yTl", BF16, [4, T], top=True)
            if s == 0:
                w_in_b = load_w_in()
            else:
                w_in_b = w_in_next[0]
            w_uq_b = alloc("w_uq_b", BF16, [2, 768], top=True)
            w_ukv_b = alloc("w_ukv_b", BF16, [8, 128], top=True)
            gate_b = alloc("gate_b", BF16, [16, 128], top=True)
            w_o_b = alloc("w_o_b", BF16, [8, D], top=True)

            def load_w_rest():
                cdma(gate_b.rearrange("p a b -> p (a b)"), gates_d[:, :])
                for c in range(2):
                    cdma(w_uq_b[:, c, :], w_uq_d[c * 128:(c + 1) * 128, :])
                cdma(w_ukv_b.rearrange("p a b -> p (a b)"), w_ukv_d[:, :])
                for c in range(8):
                    cdma(w_o_b[:, c, :], w_o_d[c * 128:(c + 1) * 128, :])

            _chk("W")
            xl = [alloc(f"xl{i}", F32, [T + 4]) for i in range(4)]
            gg = alloc("gg", BF16, [4, T])
            cqT = alloc("cqT", BF16, [2, T], top=True)
            ckvT = alloc("ckvT", BF16, [T], top=True)
            kpeT = alloc("kpeT", BF16, [T], top=True)
            xt = [alloc(f"xt{i}", F32, [D]) for i in range(4)]
            xn = alloc("xn", BF16, [4, D])
            xnT = [alloc(f"xnT{i}", BF16, [8, 512]) for i in range(2)]
            lat = alloc("lat", F32, [4, 416])
            clat = alloc("clat", BF16, [4, 384])
            kst = alloc("kst", BF16, [4, 96])
            gt = [alloc(f"gt{i}", F32, [512]) for i in range(4)]
            rt = alloc("rt", F32, [2, 64])

            for i in range(4):
                memset("pool", xl[i][:, 0:2], 0.0)
                memset("pool", xl[i][:, T + 2:T + 4], 0.0)
            memset("pool", kst[:, :, 0:64], 0.0)
            SS = stats[:, 0:4]
            RS = stats[:, 4:8]
            TM = stats[:, 8:16]
            SSQ = stats[:, 16:24]
            RSQ = stats[:, 24:32]
            def m1_xprep_ab(j):
                for b in range(4):
                    blk = j * 4 + b
                    dma(xt[b], x[s, blk * 128:(blk + 1) * 128, :])
                    act(xn[:, b, :], xt[b], AF.Square, accum=SS[:, b:b + 1])
                rstd_from_ss(RS[:, 0:4], SS[:, 0:4], stats[:, 80:84], 1.0 / D)
                for b in range(4):
                    stt(xn[:, b, :], xt[b], RS[:, b:b + 1], BC, ALU.mult, ALU.mult)

            def m1_xprep_t(j):
                xT = xnT[j % 2]
                for b in range(4):
                    bk = next_bank()
                    pb = psb_bf(bk)
                    for c in range(8):
                        tr(pb[:, c * 128:(c + 1) * 128], xn[:, b, c * 128:(c + 1) * 128], ident)
                    cp("act" if b % 2 else "dve", xT[:, :, b * 128:(b + 1) * 128],
                       pb.rearrange("p (c t) -> p c t", c=8))

            def m1_fm(j):
                xT = xnT[j % 2]
                for oc in range(8):
                    bk = next_bank()
                    pf = psb(bk)
                    for k in range(8):
                        mm(pf, w_in_b[:, k, oc * 128:(oc + 1) * 128], xT[:, k, :], k == 0, k == 7)
                    if oc < 4:
                        cp("act", xl[oc][:, 2 + j * 512:2 + (j + 1) * 512], pf)
                    else:
                        g0 = gt[(oc % 2) * 2]
                        g1 = gt[(oc % 2) * 2 + 1]
                        act(g0, pf, AF.Square)
                        stt(g0, g0, GC1 / GC2, pf, ALU.add, ALU.mult)
                        act(g1, g0, AF.Sigmoid, scale=GC2)
                        tt("dve", gg[:, oc - 4, j * 512:(j + 1) * 512], g1, pf, ALU.mult)

            def m1_tok_head(j):
                xT = xnT[j % 2]
                for b in range(4):
                    bk = next_bank()
                    pt = psb(bk)[:, 0:416]
                    for k in range(8):
                        mm(pt, xT[:, k, b * 128:(b + 1) * 128], w_in_b[:, k, 1024:1440], k == 0, k == 7)
                    cp("act", lat[:, b, :], pt)
                    act(clat[:, b, 0:256], lat[:, b, 0:256], AF.Square, accum=SSQ[:, b:b + 1])
                    act(clat[:, b, 256:384], lat[:, b, 256:384], AF.Square, accum=SSQ[:, 4 + b:5 + b])
                act(TM[:, 0:4], SSQ[:, 0:4], AF.Sqrt, bias=EPSC, scale=1.0 / 256)
                act(TM[:, 4:8], SSQ[:, 4:8], AF.Sqrt, bias=EPSC, scale=1.0 / 128)
                recip(RSQ[:, 0:8], TM[:, 0:8])
                for b in range(4):
                    blk = j * 4 + b
                    stt(clat[:, b, 0:256], lat[:, b, 0:256], RSQ[:, b:b + 1], CST[:, C_GQKVB:C_GQKVB + 256], ALU.mult, ALU.mult)
                    stt(clat[:, b, 256:384], lat[:, b, 256:384], RSQ[:, 4 + b:5 + b], CST[:, C_GQKVB + 256:C_GQKVB + 384], ALU.mult, ALU.mult)
                    kr = lat[:, b, 384:416].rearrange("p (a b) -> p a b", a=2)
                    cosb = CST[:, C_ROPE + blk * 32:C_ROPE + blk * 32 + 16].unsqueeze(1).to_broadcast([128, 2, 16])
                    sinb = CST[:, C_ROPE + blk * 32 + 16:C_ROPE + blk * 32 + 32].unsqueeze(1).to_broadcast([128, 2, 16])
                    t1 = rt[:, 0, 0:32].rearrange("p (a b) -> p a b", a=2)
                    t2 = rt[:, 1, 0:32].rearrange("p (a b) -> p a b", a=2)
                    tt("pool", t1, kr, cosb, ALU.mult)
                    tt("pool", t2, kr, sinb, ALU.mult)
                    tt("pool", kst[:, b, 64:80], t1[:, 0, :], t2[:, 1, :], ALU.subtract)
                    tt("pool", kst[:, b, 80:96], t1[:, 1, :], t2[:, 0, :], ALU.add)
                if j == 0:
                    load_w_rest()

            def m1_tok_tail(j):
                for b in range(4):
                    blk = j * 4 + b
                    bk = next_bank()
                    pb = psb_bf(bk)
                    tr(pb[:, 0:128], clat[:, b, 0:128], ident)
                    tr(pb[:, 128:256], clat[:, b, 128:256], ident)
                    tr(pb[:, 256:384], clat[:, b, 256:384], ident)
                    tr(pb[0:96, 384:512], kst[:, b, :], ident)
                    tok = slice(blk * 128, (blk + 1) * 128)
                    cp("dve", cqT[:, :, tok], pb[:, 0:256].rearrange("p (c t) -> p c t", c=2))
                    cp("dve", ckvT[:, tok], pb[:, 256:384])
                    cp("dve", kpeT[64:96, tok], pb[64:96, 384:512])

            dma(BC, cst_d[:, C_GMIXB:C_GMIXB + D])
            m1_xprep_ab(0)
            m1_xprep_t(0)
            for j in range(NTILE):
                if j + 1 < NTILE:
                    m1_xprep_ab(j + 1)
                m1_fm(j)
                if j >= 1:
                    m1_tok_tail(j - 1)
                if j + 1 < NTILE:
                    m1_xprep_t(j + 1)
                m1_tok_head(j)
            m1_tok_tail(NTILE - 1)
            for nm in ("xt0", "xt1", "xt2", "xt3", "xn", "xnT0", "xnT1", "lat", "clat", "kst", "gt0", "gt1", "gt2", "gt3", "rt", "w_in_b"):
                A.release(nm)

            _chk("M1")
            xcs = [alloc(f"xc{i}", F32, [T]) for i in range(2)]
            xcbs = [alloc(f"xcb{i}", BF16, [T]) for i in range(2)]
            HT = T // 2
            sets = [(alloc(f"ra{i}", F32, [HT]), alloc(f"a2{i}", F32, [HT]), alloc(f"iu{i}", F32, [HT])) for i in range(3)]
            lit = [0]
            h0 = alloc("h0", F32, [T])
            h1b = alloc("h1b", F32, [T])
            hs = [h0, h1b]

            def lru_conv(c):
                xc_ = xcs[c % 2]
                cw = lambda k: CST[:, C_CW + c * 4 + k:C_CW + c * 4 + k + 1]
                act(xc_, xl[c][:, 0:T], AF.Identity, bias=CST[:, C_CB + c:C_CB + c + 1], scale=cw(0))
                for k in range(1, 4):
                    stt(xc_, xl[c][:, k:k + T], cw(k), xc_, ALU.mult, ALU.add)
                cp("dve", xcbs[c % 2], xc_)
                A.release(f"xl{c}")

            lru_conv(0)
            for c in range(4):
                xc = xcs[c % 2]
                xcb = xcbs[c % 2]
                if c + 1 < 4:
                    lru_conv(c + 1)
                if c == 0:
                    sets.append((alloc("ra3", F32, [HT]), alloc("a23", F32, [HT]), alloc("iu3", F32, [HT])))
                for pair in (((0, 0), (1, 1)), ((0, 1), (1, 0))):
                    psets = [sets[(lit[0] + q) % len(sets)] for q in range(2)]
                    lit[0] += 2
                    for it, (d, hf) in enumerate(pair):
                        base = 4 * it
                        for gi in range(2):
                            for t in range(2):
                                mm(psb(base + gi * 2 + t), gate_b[:, c * 4 + d * 2 + gi, :],
                                   xcb[:, hf * HT + t * 512:hf * HT + (t + 1) * 512], True, True)
                    for it, (d, hf) in enumerate(pair):
                        ra_, a2_, iu_ = psets[it]
                        col = d * 4 + c
                        base = 4 * it
                        act(ra_.rearrange("p (a b) -> p a b", a=2), PS[:, base:base + 2, :], AF.Tanh, bias=GBAH[:, col:col + 1], scale=0.5)
                        act(iu_.rearrange("p (a b) -> p a b", a=2), PS[:, base + 2:base + 4, :], AF.Tanh, bias=GBXH[:, col:col + 1], scale=0.5)
                    for it, (d, hf) in enumerate(pair):
                        ra_, a2_, iu_ = psets[it]
                        col = d * 4 + c
                        act(a2_, ra_, AF.Exp, bias=COEF1[:, col:col + 1], scale=COEF1[:, col:col + 1])
                        act(ra_, ra_, AF.Exp, bias=COEFH[:, col:col + 1], scale=COEFH[:, col:col + 1])
                    for it, (d, hf) in enumerate(pair):
                        ra_, a2_, iu_ = psets[it]
                        act(a2_, a2_, AF.Sqrt, bias=QTR, scale=-0.25)
                    for it, (d, hf) in enumerate(pair):
                        ra_, a2_, iu_ = psets[it]
                        tsl = slice(hf * HT, (hf + 1) * HT)
                        stt(iu_, iu_, 1.0, xc[:, tsl], ALU.add, ALU.mult)
                        tt("dve", iu_, iu_, a2_, ALU.mult)
                        h = hs[d]
                        if d == 0:
                            init = 0.0 if hf == 0 else h[:, HT - 1:HT]
                            scan(h[:, tsl], ra_, iu_, init)
                        else:
                            init = 0.0 if hf == 1 else h[:, HT:HT + 1]
                            scan(h[:, tsl][:, ::-1], ra_[:, ::-1], iu_[:, ::-1], init)
                tt("dve", h0, h0, h1b, ALU.add)
                stt(yTl[:, c, :], h0, CST[:, C_GGRP + c:C_GGRP + c + 1], gg[:, c, :], ALU.mult, ALU.mult)
            for nm in ("xc0", "xc1", "xcb0", "xcb1", "ra0", "a20", "iu0", "ra1", "a21", "iu1", "ra2", "a22", "iu2", "ra3", "a23", "iu3", "h0", "h1b", "gg", "gate_b"):
                A.release(nm)

            _chk("M3")
            QT = alloc("QT", BF16, [8, T])
            KT = alloc("KT", BF16, [8, T])
            VA = alloc("VA", BF16, [NB * 8, 128])
            qsts = [alloc(f"qst{i}", BF16, [8, 96]) for i in range(2)]
            qt1s = [alloc(f"qt1{i}", F32, [8, 32]) for i in range(2)]
            qt2s = [alloc(f"qt2{i}", F32, [8, 32]) for i in range(2)]
            memset("dve", VA[:, :, 64:128], 1.0)
            for h in range(8):
                dma(KT[64:96, h, :], kpeT[64:96, :])

            def m2_head(blk):
                tok = slice(blk * 128, (blk + 1) * 128)
                qst, qt1, qt2 = qsts[blk % 2], qt1s[blk % 2], qt2s[blk % 2]
                bk = next_bank(2)
                for n in range(2):
                    for k in range(2):
                        mm(psb(bk + n)[:, 0:384], cqT[:, k, tok], w_uq_b[:, k, n * 384:(n + 1) * 384], k == 0, k == 1)
                cosq = CST[:, C_ROPEQ + blk * 32:C_ROPEQ + blk * 32 + 16]
                sinq = CST[:, C_ROPEQ + blk * 32 + 16:C_ROPEQ + blk * 32 + 32]
                for n in range(2):
                    q4 = psb(bk + n)[:, 0:384].rearrange("p (h d) -> p h d", h=4)
                    P.add("act", (lambda q4=q4, n=n, qst=qst: (lambda e: e.mul(out=qst[:, n * 4:(n + 1) * 4, 0:64], in_=q4[:, :, 0:64], mul=QSCALE)))(),
                          reads=[q4[:, :, 0:64]], writes=[qst[:, n * 4:(n + 1) * 4, 0:64]])
                    qpe = q4[:, :, 64:96].rearrange("p h (a b) -> p h a b", a=2)
                    cb_ = cosq.unsqueeze(1).unsqueeze(1).to_broadcast([128, 4, 2, 16])
                    sb_ = sinq.unsqueeze(1).unsqueeze(1).to_broadcast([128, 4, 2, 16])
                    tt("dve", qt1[:, n * 4:(n + 1) * 4, :].rearrange("p h (a b) -> p h a b", a=2), qpe, cb_, ALU.mult)
                    tt("dve", qt2[:, n * 4:(n + 1) * 4, :].rearrange("p h (a b) -> p h a b", a=2), qpe, sb_, ALU.mult)
                tt("pool", qst[:, :, 64:80], qt1[:, :, 0:16], qt2[:, :, 16:32], ALU.subtract)
                tt("pool", qst[:, :, 80:96], qt1[:, :, 16:32], qt2[:, :, 0:16], ALU.add)
                bk = next_bank()
                mm(psb(bk), ckvT[:, tok], w_ukv_b[:, :, 64:128], True, True)
                tt("dve", VA[:, blk * 8:(blk + 1) * 8, 0:64], psb(bk).rearrange("p (h d) -> p h d", h=8),
                   CST[:, C_GMLAB:C_GMLAB + 512].rearrange("p (h d) -> p h d", h=8), ALU.mult)

            def m2_tail(blk):
                tok = slice(blk * 128, (blk + 1) * 128)
                qst = qsts[blk % 2]
                bk = next_bank()
                pb = psb_bf(bk)
                for h in range(8):
                    tr(pb[0:96, h * 128:(h + 1) * 128], qst[:, h, :], ident)
                cp("act", QT[0:96, :, tok], pb[0:96, :].rearrange("p (h t) -> p h t", h=8))

            m2_head(0)
            for blk in range(NB):
                if blk + 1 < NB:
                    m2_head(blk + 1)
                m2_tail(blk)
            for t in range(NTILE):
                for h in range(8):
                    bk = next_bank()
                    mm(psb(bk)[0:64, :], w_ukv_b[:, h, 0:64], ckvT[:, t * 512:(t + 1) * 512], True, True)
                    cp("act" if h % 2 else "dve", KT[0:64, h, t * 512:(t + 1) * 512], psb(bk)[0:64, :])
            for nm in ("qst0", "qst1", "qt10", "qt11", "qt20", "qt21", "cqT", "ckvT", "kpeT", "w_uq_b", "w_ukv_b"):
                A.release(nm)

            _chk("M2")
            yTm = alloc("yTm", BF16, [4, T], top=True)
            PT = [alloc(f"PT{i}", BF16, [1024]) for i in range(4)]
            rden = alloc("rden", F32, [1024])
            it = 0
            for h in range(8):
                for qh in range(2):
                    obk = 4 + 2 * (it % 2)
                    q0 = qh * 1024

                    def s_mm(kb, sb0):
                        for t in range(2):
                            mm(psb(sb0 + t), KT[0:96, h, kb * 128:(kb + 1) * 128],
                               QT[0:96, h, q0 + t * 512:q0 + (t + 1) * 512], True, True)
                    s_mm(0, 0)
                    for kb in range(NB):
                        sb0 = 2 * (kb % 2)
                        if kb + 1 < NB:
                            s_mm(kb + 1, 2 * ((kb + 1) % 2))
                        pt = PT[kb % 4]
                        act(pt.rearrange("p (a b) -> p a b", a=2), PS[:, sb0:sb0 + 2, :], AF.Exp)
                        for t in range(2):
                            mm(psb(obk + t), VA[:, kb * 8 + h, :], pt[:, t * 512:(t + 1) * 512], kb == 0, kb == NB - 1)
                    ob = PS[:, obk:obk + 2, :]
                    recip(rden[0:64, :].rearrange("p (a b) -> p a b", a=2), ob[64:128, :, :])
                    po = (h % 2) * 64
                    tt("dve", yTm[po:po + 64, h // 2, q0:q0 + 1024].rearrange("p (a b) -> p a b", a=2),
                       ob[0:64, :, :], rden[0:64, :].rearrange("p (a b) -> p a b", a=2), ALU.mult)
                    it += 1
            for nm in ("PT0", "PT1", "PT2", "PT3", "rden", "QT", "KT", "VA"):
                A.release(nm)

            _chk("M4")
            hnT = alloc("hnT", BF16, [8, T])
            NJA = 19
            wupb = [alloc(f"wupb{i}", BF16, [8, 256]) for i in range(2)]
            wdbA = alloc("wdbA", BF16, [NJA, D])
            w_up_v = w_up_d.rearrange("(c p) n -> p c n", p=128)

            def load_wup(j):
                wb = wupb[j % 2]
                for part in range(2):
                    col = part * DFF + j * 128
                    cdma(wb[:, :, part * 128:(part + 1) * 128], w_up_v[:, :, col:col + 128])

            load_wup(0)
            load_wup(1)
            for j in range(NJA):
                cdma(wdbA[:, j, :], w_dn_d[j * 128:(j + 1) * 128, :])
            sqs = [alloc(f"sq{i}", BF16, [8, 512]) for i in range(2)]
            gst = alloc("gst", F32, [NB * 2])
            grs = alloc("grs", F32, [NB * 2])
            xr = [alloc(f"xr{i}", F32, [D]) for i in range(2)]
            h1t = [alloc(f"h1t{i}", F32, [D]) for i in range(2)]
            hn = alloc("hn", BF16, [D])
            junk = alloc("junk", BF16, [D])
            sbk = next_bank()
            for jt in range(NTILE):
                sq4 = sqs[jt % 2]
                tok4 = slice(jt * 512, (jt + 1) * 512)
                tt("dve", sq4[:, 0:4, :], yTl[:, :, tok4], yTl[:, :, tok4], ALU.mult)
                tt("dve", sq4[:, 4:8, :], yTm[:, :, tok4], yTm[:, :, tok4], ALU.mult)
                for b4 in range(4):
                    blk = jt * 4 + b4
                    for g in range(2):
                        col = blk * 2 + g
                        for c in range(4):
                            mm(psb(sbk)[:, col:col + 1], sq4[:, g * 4 + c, b4 * 128:(b4 + 1) * 128],
                               ig2[:, g * 4 + c:g * 4 + c + 1], c == 0, c == 3)
            cp("dve", gst, psb(sbk)[:, 0:NB * 2])
            act(gst, gst, AF.Sqrt, bias=EPSC, scale=1.0 / 512)
            recip(grs, gst)
            H1S = stats[:, 32:48]
            H1R = stats[:, 48:64]
            H1T = stats[:, 64:80]
            def m5_mm_half(i):
                blk, n = divmod(i, 2)
                tok = slice(blk * 128, (blk + 1) * 128)
                if n == 0:
                    dma(xr[blk % 2], x[s, tok, :])
                gb = 2 * (i % 3)
                for g in range(2):
                    for c in range(4):
                        mm(psb(gb + g), (yTl if g == 0 else yTm)[:, c, tok], w_o_b[:, g * 4 + c, n * 512:(n + 1) * 512], c == 0, c == 3)

            def m5_stt_half(i):
                blk, n = divmod(i, 2)
                xb = xr[blk % 2]
                hb = h1t[blk % 2]
                gb = 2 * (i % 3)
                stt(hb[:, n * 512:(n + 1) * 512], psb(gb), grs[:, blk * 2:blk * 2 + 1],
                    xb[:, n * 512:(n + 1) * 512], ALU.mult, ALU.add)
                stt(hb[:, n * 512:(n + 1) * 512], psb(gb + 1), grs[:, blk * 2 + 1:blk * 2 + 2],
                    hb[:, n * 512:(n + 1) * 512], ALU.mult, ALU.add)

            def m5_b(blk):
                tok = slice(blk * 128, (blk + 1) * 128)
                hb = h1t[blk % 2]
                rstd_from_ss(H1R[:, blk:blk + 1], H1S[:, blk:blk + 1], H1T[:, blk:blk + 1], 1.0 / D)
                stt(hn, hb, H1R[:, blk:blk + 1], BC, ALU.mult, ALU.mult)
                pb = psb_bf(6 + blk % 2)
                for c in range(8):
                    tr(pb[:, c * 128:(c + 1) * 128], hn[:, c * 128:(c + 1) * 128], ident)
                cp("act", hnT[:, :, tok], pb.rearrange("p (c t) -> p c t", c=8))

            dma(BC, cst_d[:, C_GFFN:C_GFFN + D])
            m5_mm_half(0)
            m5_mm_half(1)
            for i in range(2 * NB):
                blk, n = divmod(i, 2)
                tok = slice(blk * 128, (blk + 1) * 128)
                if i + 2 < 2 * NB:
                    m5_mm_half(i + 2)
                m5_stt_half(i)
                if n == 1:
                    hb = h1t[blk % 2]
                    dma(out[s, tok, :], hb)
                    act(junk, hb, AF.Square, accum=H1S[:, blk:blk + 1])
                elif blk >= 1:
                    m5_b(blk - 1)
            m5_b(NB - 1)
            bank_rr[0] = 0
            for nm in ("sq0", "sq1", "gst", "grs", "xr0", "xr1", "h1t0", "h1t1", "hn", "junk", "yTl", "yTm", "w_o_b"):
                A.release(nm)

            _chk("M5")
            actT = alloc("actT", BF16, [NJ, T])
            accg = alloc("accg", F32, [T])
            accv = alloc("accv", F32, [T])
            sgl = alloc("sgl", F32, [T])
            for j in range(NJ):
                wb = wupb[j % 2]
                if 1 <= j and j + 1 < NJ:
                    load_wup(j + 1)
                for part in range(2):
                    b0 = part * 4
                    for t in range(4):
                        for k in range(8):
                            mm(psb(b0 + t), wb[:, k, part * 128:(part + 1) * 128], hnT[:, k, t * 512:(t + 1) * 512], k == 0, k == 7)
                    jj = part * NJ + j
                    fw = lambda k: CST[:, C_FW + jj * 3 + k:C_FW + jj * 3 + k + 1]
                    acc = accg if part == 0 else accv
                    u = PS[:, b0:b0 + 4, :].rearrange("p a b -> p (a b)")
                    act(acc.rearrange("p (a b) -> p a b", a=4), PS[:, b0:b0 + 4, :], AF.Identity,
                        bias=CST[:, C_FB + jj:C_FB + jj + 1], scale=fw(1))
                    stt(acc[:, 1:T], u[:, 0:T - 1], fw(0), acc[:, 1:T], ALU.mult, ALU.add)
                    stt(acc[:, 0:T - 1], u[:, 1:T], fw(2), acc[:, 0:T - 1], ALU.mult, ALU.add)
                act(sgl, accg, AF.Silu)
                tt("dve", actT[:, j, :], sgl, accv, ALU.mult)
            for nm in ("wupb0", "wupb1", "accg", "accv", "sgl", "hnT"):
                A.release(nm)

            if s + 1 < NSEQ:
                w_in_next[0] = alloc("w_in_b", BF16, [8, INC])
            dma(BC, cst_d[:, C_FG:C_FG + D])
            wdbB = alloc("wdbB", BF16, [NJ - NJA, D])
            for j in range(NJA, NJ):
                cdma(wdbB[:, j - NJA, :], w_dn_d[j * 128:(j + 1) * 128, :])
            hr = [alloc(f"hr{i}", F32, [D]) for i in range(2)]
            yo = [alloc(f"yo{i}", F32, [D]) for i in range(2)]
            junk = alloc("junk", BF16, [D])
            FS = stats[:, 32:48]
            FR = stats[:, 48:64]
            FT = stats[:, 64:80]
            for blk in range(NB):
                tok = slice(blk * 128, (blk + 1) * 128)
                hb = hr[blk % 2]
                yb = yo[blk % 2]
                dma(hb, out[s, tok, :])
                bk = next_bank(2)
                for n in range(2):
                    for j in range(NJ):
                        mm(psb(bk + n), actT[:, j, tok], (wdbA[:, j, n * 512:(n + 1) * 512] if j < NJA else wdbB[:, j - NJA, n * 512:(n + 1) * 512]), j == 0, j == NJ - 1)
                for n in range(2):
                    tt("dve", yb[:, n * 512:(n + 1) * 512], psb(bk + n), hb[:, n * 512:(n + 1) * 512], ALU.add)
                act(junk, yb, AF.Square, accum=FS[:, blk:blk + 1])
                rstd_from_ss(FR[:, blk:blk + 1], FS[:, blk:blk + 1], FT[:, blk:blk + 1], 1.0 / D)
                stt(yb, yb, FR[:, blk:blk + 1], BC, ALU.mult, ALU.mult)
                dma(out[s, tok, :], yb)
                if blk == 3 and s + 1 < NSEQ:
                    load_w_in(w_in_next[0])
            for nm in ("wdbA", "wdbB", "hr0", "hr1", "yo0", "yo1", "junk", "actT"):
                A.release(nm)

        P.finalize()
        import contextlib
        with contextlib.ExitStack() as es:
            csems = {e: es.enter_context(nc.semaphore(f"c_{e}")) for e in Prog.CE}
            dsems = [es.enter_context(nc.semaphore(f"d_{i}")) for i in range(P.ndsem)]
            block = es.enter_context(nc.Block())

            @block.sync
            def _(eng):
                P.emit_engine("sp", eng, csems, dsems)

            @block.tensor
            def _(eng):
                P.emit_engine("pe", eng, csems, dsems)

            @block.scalar
            def _(eng):
                P.emit_engine("act", eng, csems, dsems)

            @block.vector
            def _(eng):
                P.emit_engine("dve", eng, csems, dsems)

            @block.gpsimd
            def _(eng):
                P.emit_engine("pool", eng, csems, dsems)
    return nc


def _pc(v, nchunk):
    return np.ascontiguousarray(np.asarray(v, np.float32).reshape(nchunk, 128).T)


def _build_consts(inp):
    cst = np.zeros((128, NCST), np.float32)
    cw = np.asarray(inp["lru_conv_w"], np.float32)[0]
    cst[:, C_CW:C_CW + 16] = cw.reshape(4, 4, 128).transpose(2, 1, 0).reshape(128, 16)
    cst[:, C_CB:C_CB + 4] = _pc(np.asarray(inp["lru_conv_b"])[0], 4)
    for name, off in (("lru_gate_a_b", C_GBA), ("lru_gate_x_b", C_GBX), ("lru_lambda", C_LAM)):
        v = np.asarray(inp[name], np.float32)[0]
        cst[:, off:off + 8] = v.reshape(2, 4, 128).transpose(2, 0, 1).reshape(128, 8)
    fw = np.asarray(inp["ffn_conv_w"], np.float32)[0]
    cst[:, C_FW:C_FW + 132] = fw.reshape(3, 44, 128).transpose(2, 1, 0).reshape(128, 132)
    cst[:, C_FB:C_FB + 44] = _pc(np.asarray(inp["ffn_conv_b"])[0], 44)
    cst[:, C_GMIX:C_GMIX + 8] = _pc(np.asarray(inp["ln_mix_g"])[0], 8)
    cst[:, C_GQ:C_GQ + 2] = _pc(np.asarray(inp["q_norm_g"])[0], 2)
    cst[:, C_GKV:C_GKV + 1] = _pc(np.asarray(inp["kv_norm_g"])[0], 1)
    ggrp = np.concatenate([np.asarray(inp["grp_norm_lru_g"], np.float32)[0], np.asarray(inp["grp_norm_mla_g"], np.float32)[0]])
    cst[:, C_GGRP:C_GGRP + 8] = _pc(ggrp, 8)
    half = 16
    freqs = (np.float32(10000.0) ** (-(np.arange(half, dtype=np.float32) / np.float32(half)))).astype(np.float32)
    pos = np.arange(T, dtype=np.float32)
    ang = (pos[:, None] * freqs[None, :]).astype(np.float32)
    cs = np.concatenate([np.cos(ang), np.sin(ang)], axis=1).astype(np.float32)
    tab = cs.reshape(NB, 128, 32).transpose(1, 0, 2).reshape(128, NB * 32)
    cst[:, C_ROPE:C_ROPE + 512] = tab
    cst[:, C_ROPEQ:C_ROPEQ + 512] = tab * np.float32(QSCALE)
    cst[:, C_FG:C_FG + D] = np.broadcast_to(np.asarray(inp["final_norm_g"], np.float32)[None, :], (128, D))
    cst[:, C_GFFN:C_GFFN + D] = np.broadcast_to(np.asarray(inp["ln_ffn_g"], np.float32)[0][None, :], (128, D))
    cst[:, C_IDENT:C_IDENT + 128] = np.eye(128, dtype=np.float32)
    cst[:, C_GMIXB:C_GMIXB + D] = np.broadcast_to(np.asarray(inp["ln_mix_g"], np.float32)[0][None, :], (128, D))
    gqkv = np.concatenate([np.asarray(inp["q_norm_g"], np.float32)[0], np.asarray(inp["kv_norm_g"], np.float32)[0]])
    cst[:, C_GQKVB:C_GQKVB + 384] = np.broadcast_to(gqkv[None, :], (128, 384))
    cst[:, C_GMLAB:C_GMLAB + 512] = np.broadcast_to(np.asarray(inp["grp_norm_mla_g"], np.float32)[0][None, :], (128, 512))
    return cst


def _build_gates(inp):
    ga = np.asarray(inp["lru_gate_a_w"], np.float32)[0]
    gx = np.asarray(inp["lru_gate_x_w"], np.float32)[0]
    g = np.zeros((128, 4, 4, 128), np.float32)
    for c in range(4):
        for d in range(2):
            for gi, w in enumerate((ga, gx)):
                for hh in range(2):
                    g[hh * 64:(hh + 1) * 64, c, d * 2 + gi, hh * 64:(hh + 1) * 64] = w[d, 2 * c + hh]
    return g.reshape(128, 2048)


_NC_CACHE = {}


def kernel(**inputs):
    inp = {k: np.asarray(v) for k, v in inputs.items()}
    n = 8
    xfull = np.ascontiguousarray(inp["x"], dtype=np.float32)
    cst = _build_consts(inp)
    gates = _build_gates(inp)
    shared = {
        "cst": cst,
        "w_in": np.ascontiguousarray(inp["w_in"][0], np.float32),
        "gates": gates,
        "w_uq": np.ascontiguousarray(inp["w_uq"][0].reshape(256, 768), np.float32),
        "w_ukv": np.ascontiguousarray(inp["w_ukv"][0].reshape(128, 1024), np.float32),
        "w_o": np.ascontiguousarray(inp["w_o"][0], np.float32),
        "w_up": np.ascontiguousarray(inp["w_up"][0], np.float32),
        "w_down": np.ascontiguousarray(inp["w_down"][0], np.float32),
    }
    if "nc" not in _NC_CACHE:
        _NC_CACHE["nc"] = build_program()
    nc = _NC_CACHE["nc"]
    in_maps = []
    for c in range(n):
        m = dict(shared)
        m["x"] = np.ascontiguousarray(xfull[c * NSEQ:(c + 1) * NSEQ])
        in_maps.append(m)
    res = run_bass_kernel_spmd(nc, in_maps, core_ids=list(range(n)))
    outs = [np.asarray(r["out"], dtype=np.float32) for r in res.results]
    return np.concatenate(outs, axis=0)
```

```python
import math
import numpy as np
import concourse.bass as bass
import concourse.mybir as mybir
from concourse.bass_utils import run_bass_kernel_spmd

F32 = mybir.dt.float32
BF16 = mybir.dt.bfloat16
ALU = mybir.AluOpType
AF = mybir.ActivationFunctionType

D = 1024
T = 2048
NSEQ = 2
NB = T // 128
NTILE = T // 512
INC = 1440
DFF = 2816
NJ = DFF // 128
EPS = 1e-6
QSCALE = 1.0 / math.sqrt(96.0)
GC1 = 2.0 * math.sqrt(2.0 / math.pi)
GC2 = GC1 * 0.044715

_o = 0
def _c(n):
    global _o
    s = _o
    _o += n
    return s
C_CW = _c(16)
C_CB = _c(4)
C_GBA = _c(8)
C_GBX = _c(8)
C_LAM = _c(8)
C_FW = _c(132)
C_FB = _c(44)
C_GMIX = _c(8)
C_GQ = _c(2)
C_GKV = _c(1)
C_GGRP = _c(8)
C_ROPE = _c(512)
C_ROPEQ = _c(512)
C_IDENT = _c(128)
C_GQKVB = _c(384)
C_GMLAB = _c(512)
NCST_SB = _o
C_FG = _c(1024)
C_GFFN = _c(1024)
C_GMIXB = _c(1024)
NCST = _o

_DTSIZE = {F32: 4, BF16: 2}


def _prod(xs):
    r = 1
    for v in xs:
        r *= int(v)
    return r


class _Op:
    __slots__ = ("eng", "emit", "deps", "id", "idx", "sig", "val", "dma", "dsem", "dval", "raw_same")


def _intervals(f0, dims, es):
    lo = f0
    hi = f0
    for st, n in dims:
        if st >= 0:
            hi += st * (n - 1)
        else:
            lo += st * (n - 1)
    hi += 1
    if len(dims) >= 2:
        st0, n0 = dims[0]
        inner_lo = 0
        inner_hi = 0
        for st, n in dims[1:]:
            if st >= 0:
                inner_hi += st * (n - 1)
            else:
                inner_lo += st * (n - 1)
        inner_hi += 1
        ext = inner_hi - inner_lo
        if 1 < n0 <= 64 and abs(st0) > 2 * ext:
            out = []
            for i in range(n0):
                out.extend(_intervals(f0 + st0 * i, dims[1:], es))
            return out
    return [(lo * es, hi * es)]


def region(ap):
    t = ap.tensor
    es = _DTSIZE[ap.dtype]
    dims = [(int(a), int(b)) for a, b in ap.ap]
    shp = [int(s) for s in t.shape]
    name = t.name
    if name in ("sb", "ps"):
        row = _prod(shp[1:])
        off = int(ap.offset)
        p0 = off // row
        f0 = off % row
        pn = dims[0][1]
        if name == "ps":
            q0 = p0 // 32 * 32
            q1 = (p0 + pn + 31) // 32 * 32
            return [(name, q0, q1, lo // 2048 * 2048, (hi + 2047) // 2048 * 2048)
                    for lo, hi in _intervals(f0, dims[1:], es)]
        return [(name, p0, p0 + pn, lo, hi) for lo, hi in _intervals(f0, dims[1:], es)]
    return [(name, 0, 1) + _bound(int(ap.offset), dims, es)]


def _bound(f0, dims, es):
    lo = f0
    hi = f0
    for st, n in dims:
        if st >= 0:
            hi += st * (n - 1)
        else:
            lo += st * (n - 1)
    return (lo * es, (hi + 1) * es)


class Prog:
    CE = ("pe", "act", "dve", "pool")

    def __init__(self):
        self.ops = []
        self.by_eng = {e: [] for e in self.CE + ("sp",)}
        self.recs = {}
        self.ndsem = 32
        self.dcount = [0] * self.ndsem
        self.dlast = [None] * self.ndsem
        self.dnext = {"sp": 0, "pool": 0}
        self.stopped = False

    def _scan(self, space, p0, p1, lo, hi, kinds):
        out = []
        for r in self.recs.get(space, ()):
            if r[4] in kinds and r[0] < p1 and p0 < r[1] and r[2] < hi and lo < r[3]:
                out.append(r)
        return out

    def add(self, eng, emit, reads=(), writes=(), track_dram=("out",), dma=False):
        if self.stopped:
            return None
        op = _Op()
        op.eng = eng
        op.emit = emit
        op.id = len(self.ops)
        op.deps = {}
        op.sig = False
        op.dma = (eng == "sp") or dma
        op.raw_same = set()
        self.ops.append(op)
        rregs = []
        wregs = []
        for ap in reads:
            for rg in region(ap):
                if rg[0] in ("sb", "ps") or rg[0] in track_dram:
                    rregs.append(rg)
        for ap in writes:
            for rg in region(ap):
                if rg[0] in ("sb", "ps") or rg[0] in track_dram:
                    wregs.append(rg)
        for (sp, p0, p1, lo, hi) in rregs:
            for r in self._scan(sp, p0, p1, lo, hi, "w"):
                op.deps[r[5]] = "raw"
            if sp == "ps":
                for r in self._scan(sp, p0, p1, lo, hi, "r"):
                    if self.ops[r[5]].eng != eng and r[5] not in op.deps:
                        op.deps[r[5]] = "rar"
        for (sp, p0, p1, lo, hi) in wregs:
            for r in self._scan(sp, p0, p1, lo, hi, "wr"):
                if r[5] not in op.deps:
                    op.deps[r[5]] = "war" if r[4] == "r" else "waw"
        for (sp, p0, p1, lo, hi) in wregs:
            lst = self.recs.setdefault(sp, [])
            lst[:] = [r for r in lst if not (p0 <= r[0] and r[1] <= p1 and lo <= r[2] and r[3] <= hi)]
            lst.append([p0, p1, lo, hi, "w", op.id])
        for (sp, p0, p1, lo, hi) in rregs:
            lst = self.recs.setdefault(sp, [])
            lst[:] = [r for r in lst if not (r[4] == "r" and self.ops[r[5]].eng == eng and not self.ops[r[5]].dma
                                             and p0 <= r[0] and r[1] <= p1 and lo <= r[2] and r[3] <= hi)]
            lst.append([p0, p1, lo, hi, "r", op.id])
        op.deps.pop(op.id, None)
        if op.dma:
            half = self.ndsem // 2
            k = self.dnext[eng] + (0 if eng == "sp" else half)
            self.dnext[eng] = (self.dnext[eng] + 1) % half
            op.dsem = k
            self.dcount[k] += 1
            op.dval = 16 * self.dcount[k]
            if self.dlast[k] is not None:
                op.deps.setdefault(self.dlast[k], "sem")
            self.dlast[k] = op.id
        self.by_eng[eng].append(op)
        return op

    def finalize(self):
        ops = self.ops
        for op in ops:
            need = {}
            for d, kind in op.deps.items():
                dop = ops[d]
                if dop.eng == op.eng and not dop.dma:
                    if op.eng == "pe":
                        continue
                need[d] = kind
            op.deps = need
            for d in need:
                ops[d].sig = True
        cnt = {e: 0 for e in self.CE}
        for op in ops:
            if not op.dma and op.sig:
                cnt[op.eng] += 1
                op.val = cnt[op.eng]
        clk = {e: {"c": {x: 0 for x in self.CE}, "d": [0] * self.ndsem} for e in self.CE + ("sp",)}
        snap = {}
        for op in ops:
            ck = clk[op.eng]
            waits = []
            for d in sorted(op.deps):
                dop = ops[d]
                if dop.dma:
                    if ck["d"][dop.dsem] < dop.dval:
                        waits.append(("d", dop.dsem, dop.dval))
                        ck["d"][dop.dsem] = dop.dval
                        self._merge(ck, snap[d])
                else:
                    if ck["c"][dop.eng] < dop.val:
                        waits.append(("c", dop.eng, dop.val))
                        ck["c"][dop.eng] = dop.val
                        self._merge(ck, snap[d])
            best = {}
            for k, s, v in waits:
                if (k, s) not in best or best[(k, s)] < v:
                    best[(k, s)] = v
            op.raw_same = best
            snap[op.id] = {"c": dict(ck["c"]), "d": list(ck["d"])}

    @staticmethod
    def _merge(ck, sn):
        for e, v in sn["c"].items():
            if ck["c"][e] < v:
                ck["c"][e] = v
        dd = ck["d"]
        sd = sn["d"]
        for i in range(len(dd)):
            if dd[i] < sd[i]:
                dd[i] = sd[i]

    def emit_engine(self, ename, eng, csems, dsems):
        for op in self.by_eng[ename]:
            for (k, s), v in op.raw_same.items():
                if k == "c":
                    eng.wait_ge(csems[s], v)
                else:
                    eng.wait_ge(dsems[s], v)
            ins = op.emit(eng)
            if op.dma:
                ins.then_inc(dsems[op.dsem], 16)
            elif op.sig:
                ins.then_inc(csems[op.eng], 1)
        if ename == "sp":
            for k in range(self.ndsem):
                if self.dcount[k]:
                    eng.wait_ge(dsems[k], 16 * self.dcount[k])


class Arena:
    def __init__(self, nbytes):
        self.free = [(0, nbytes)]
        self.live = {}

    def alloc(self, name, nbytes, top=False):
        nbytes = (nbytes + 63) // 64 * 64
        if top:
            for i in range(len(self.free) - 1, -1, -1):
                o, n = self.free[i]
                if n >= nbytes:
                    if n == nbytes:
                        self.free.pop(i)
                    else:
                        self.free[i] = (o, n - nbytes)
                    self.live[name] = (o + n - nbytes, nbytes)
                    return o + n - nbytes
            raise RuntimeError(f"arena OOM(top) for {name} ({nbytes}); free={self.free} live={self.live}")
        for i, (o, n) in enumerate(self.free):
            if n >= nbytes:
                if n == nbytes:
                    self.free.pop(i)
                else:
                    self.free[i] = (o + nbytes, n - nbytes)
                self.live[name] = (o, nbytes)
                return o
        raise RuntimeError(f"arena OOM for {name} ({nbytes}); free={self.free} live={self.live}")

    def release(self, name):
        o, n = self.live.pop(name)
        self.free.append((o, n))
        self.free.sort()
        m = []
        for o, n in self.free:
            if m and m[-1][0] + m[-1][1] == o:
                m[-1] = (m[-1][0], m[-1][1] + n)
            else:
                m.append((o, n))
        self.free = m


SB_WORDS = 52992
LIMIT = [None]
CAST_ENG = "dve"
import os
SKIP_KST = bool(os.environ.get("SKIP_KST"))
DUMP = []
DUMP_LAYOUT = {}
NDBG = 24576


class _Stop(Exception):
    pass


def build_program():
    nc = bass.Bass("TRN2", target_bir_lowering=False)
    x = nc.dram_tensor("x", [NSEQ, T, D], F32, kind="ExternalInput").ap()
    cst_d = nc.dram_tensor("cst", [128, NCST], F32, kind="ExternalInput").ap()
    w_in_d = nc.dram_tensor("w_in", [D, INC], F32, kind="ExternalInput").ap()
    gates_d = nc.dram_tensor("gates", [128, 2048], F32, kind="ExternalInput").ap()
    w_uq_d = nc.dram_tensor("w_uq", [256, 768], F32, kind="ExternalInput").ap()
    w_ukv_d = nc.dram_tensor("w_ukv", [128, 1024], F32, kind="ExternalInput").ap()
    w_o_d = nc.dram_tensor("w_o", [D, D], F32, kind="ExternalInput").ap()
    w_up_d = nc.dram_tensor("w_up", [D, 2 * DFF], F32, kind="ExternalInput").ap()
    w_dn_d = nc.dram_tensor("w_down", [DFF, D], F32, kind="ExternalInput").ap()
    out = nc.dram_tensor("out", [NSEQ, T, D], F32, kind="ExternalOutput").ap()
    dbg = nc.dram_tensor("dbg", [128, NDBG], F32, kind="ExternalOutput").ap() if LIMIT[0] else None

    P = Prog()
    A = Arena(SB_WORDS * 4)

    with (
        nc.sbuf_tensor("sb", [128, SB_WORDS], F32) as SB,
        nc.psum_tensor("ps", [128, 8, 512], F32) as PS,
    ):
        def view(off, dt, shape):
            n = _prod(shape)
            es = _DTSIZE[dt]
            assert off % 4 == 0 and (n * es) % 4 == 0
            ap = SB[:, off // 4:(off + n * es) // 4]
            if dt != F32:
                ap = ap.bitcast(dt)
            if len(shape) == 2:
                ap = ap.rearrange("p (a b) -> p a b", a=shape[0])
            elif len(shape) == 3:
                ap = ap.rearrange("p (a b c) -> p a b c", a=shape[0], b=shape[1])
            return ap

        def alloc(name, dt, shape, top=False):
            off = A.alloc(name, _prod(shape) * _DTSIZE[dt], top)
            return view(off, dt, shape)

        def psb(b):
            return PS[:, b, :]

        def psb_bf(b):
            return PS[:, b, :].bitcast(BF16)

        def dma(o, i):
            P.add("sp", lambda e: e.dma_start(out=o, in_=i), reads=[i], writes=[o])

        def cdma(o, i):
            P.add("pool", lambda e: e.dma_start(out=o, in_=i), reads=[i], writes=[o], dma=True)

        def act(o, i, func, bias=None, scale=None, accum=None, extra_reads=()):
            kw = {}
            rd = [i] + list(extra_reads)
            if bias is not None:
                kw["bias"] = bias
                if not isinstance(bias, float):
                    rd.append(bias)
            if scale is not None:
                kw["scale"] = scale
                if not isinstance(scale, float):
                    rd.append(scale)
            wr = [o]
            if accum is not None:
                kw["accum_out"] = accum
                wr.append(accum)
            P.add("act", lambda e: e.activation(out=o, in_=i, func=func, **kw), reads=rd, writes=wr)

        def tt(eng, o, a, b, op):
            P.add(eng, lambda e: e.tensor_tensor(out=o, in0=a, in1=b, op=op), reads=[a, b], writes=[o])

        def ts(eng, o, a, s1, s2, op0, op1=None):
            rd = [a] + [s for s in (s1, s2) if s is not None and not isinstance(s, float)]
            if op1 is None:
                P.add(eng, lambda e: e.tensor_scalar(out=o, in0=a, scalar1=s1, scalar2=None, op0=op0), reads=rd, writes=[o])
            else:
                P.add(eng, lambda e: e.tensor_scalar(out=o, in0=a, scalar1=s1, scalar2=s2, op0=op0, op1=op1), reads=rd, writes=[o])

        def stt(o, a, s, b, op0, op1):
            rd = [a, b] + ([] if isinstance(s, float) else [s])
            P.add("dve", lambda e: e.scalar_tensor_tensor(out=o, in0=a, scalar=s, in1=b, op0=op0, op1=op1), reads=rd, writes=[o])

        def cp(eng, o, i):
            if eng == "act":
                P.add("act", lambda e: e.copy(out=o, in_=i), reads=[i], writes=[o])
            else:
                P.add(eng, lambda e: e.tensor_copy(out=o, in_=i), reads=[i], writes=[o])

        def scast(eng, o, i, sc):
            if eng == "act":
                act(o, i, AF.Copy, scale=sc)
            else:
                ts(eng, o, i, sc, None, ALU.mult)

        def recip(o, i):
            P.add("dve", lambda e: e.reciprocal(out=o, in_=i), reads=[i], writes=[o])

        def mm(o, l, r, start, stop):
            P.add("pe", lambda e: e.matmul(o, l, r, start=start, stop=stop), reads=[l, r], writes=[o])

        def tr(o, i, ident):
            P.add("pe", lambda e: e.transpose(o, i, ident), reads=[i, ident], writes=[o])

        def memset(eng, o, v):
            P.add(eng, lambda e: e.memset(o, v), reads=[], writes=[o])

        def scan(o, d0, d1, init=0.0):
            rd = [d0, d1] + ([] if isinstance(init, float) else [init])
            P.add("dve", lambda e: e.tensor_tensor_scan(out=o, data0=d0, data1=d1, initial=init, op0=ALU.mult, op1=ALU.add),
                  reads=rd, writes=[o])

        def rstd_from_ss(o, ss, tmp, inv_n):
            n = int(tmp.shape[-1])
            act(tmp, ss, AF.Sqrt, bias=EPSC, scale=inv_n)
            recip(o, tmp)

        CST = alloc("cst", F32, [NCST_SB])
        BC = alloc("bc", F32, [D])
        ident = alloc("ident", BF16, [128])
        ones_bf = alloc("ones", BF16, [2])
        small = alloc("small", F32, [64])
        COEFH = small[:, 0:8]
        COEF1 = small[:, 8:16]
        GBAH = small[:, 16:24]
        GBXH = small[:, 24:32]
        QTR = small[:, 32:33]
        EPSC = small[:, 33:34]
        stats = alloc("stats", F32, [96])
        ig2 = alloc("ig2", BF16, [8])

        dma(CST, cst_d[:, 0:NCST_SB])
        cp("dve", ident, CST[:, C_IDENT:C_IDENT + 128])
        memset("pool", ones_bf, 1.0)
        memset("pool", QTR, 0.25)
        memset("pool", EPSC, EPS)
        tmp8 = small[:, 40:48]
        act(tmp8, CST[:, C_LAM:C_LAM + 8], AF.Exp, scale=-1.0)
        ts("dve", tmp8, tmp8, 1.0, None, ALU.add)
        act(tmp8, tmp8, AF.Ln)
        ts("dve", COEFH, tmp8, -4.0, None, ALU.mult)
        ts("dve", COEF1, tmp8, -8.0, None, ALU.mult)
        ts("dve", GBAH, CST[:, C_GBA:C_GBA + 8], 0.5, None, ALU.mult)
        ts("dve", GBXH, CST[:, C_GBX:C_GBX + 8], 0.5, None, ALU.mult)
        tmpg = small[:, 48:56]
        tt("dve", tmpg, CST[:, C_GGRP:C_GGRP + 8], CST[:, C_GGRP:C_GGRP + 8], ALU.mult)
        recip(tmpg, tmpg)
        cp("dve", ig2, tmpg)

        bank_rr = [0]

        def next_bank(n=1):
            b = bank_rr[0]
            if b + n > 8:
                b = 0
            bank_rr[0] = (b + n) % 8
            return b

        def next_bank4():
            b = 0 if bank_rr[0] in (0, 5, 6, 7) else 4
            bank_rr[0] = (b + 4) % 8
            return b

        def _chk(name):
            if LIMIT[0] == name and not P.stopped:
                pos = 0
                for nm in DUMP:
                    o, nbytes = A.live[nm]
                    w = nbytes // 4
                    dma(dbg[:, pos:pos + w], SB[:, o // 4:o // 4 + w])
                    DUMP_LAYOUT[nm] = (pos, w)
                    pos += w
                P.stopped = True

        def load_w_in(wb_=None):
            if wb_ is None:
                wb_ = alloc("w_in_b", BF16, [8, INC], top=True)
            for c in range(8):
                cdma(wb_[:, c, :], w_in_d[c * 128:(c + 1) * 128, :])
            return wb_

        w_in_next = [None]

        for s in range(NSEQ):
            yTl = alloc("yTl", BF16, [4, T], top=True)
            if s == 0:
                w_in_b = load_w_in()
            else:
                w_in_b = w_in_next[0]
            w_uq_b = alloc("w_uq_b", BF16, [2, 768], top=True)
            w_ukv_b = alloc("w_ukv_b", BF16, [8, 128], top=True)
            gate_b = alloc("gate_b", BF16, [16, 128], top=True)
            w_o_b = alloc("w_o_b", BF16, [8, D], top=True)

            def load_w_rest():
                cdma(gate_b.rearrange("p a b -> p (a b)"), gates_d[:, :])
                for c in range(2):
                    cdma(w_uq_b[:, c, :], w_uq_d[c * 128:(c + 1) * 128, :])
                cdma(w_ukv_b.rearrange("p a b -> p (a b)"), w_ukv_d[:, :])
                for c in range(8):
                    cdma(w_o_b[:, c, :], w_o_d[c * 128:(c + 1) * 128, :])

            _chk("W")
            xl = [alloc(f"xl{i}", F32, [T + 4]) for i in range(4)]
            gg = alloc("gg", BF16, [4, T])
            cqT = alloc("cqT", BF16, [2, T], top=True)
            ckvT = alloc("ckvT", BF16, [T], top=True)
            kpeT = alloc("kpeT", BF16, [T], top=True)
            xt = [alloc(f"xt{i}", F32, [D]) for i in range(4)]
            xn = alloc("xn", BF16, [4, D])
            xnT = [alloc(f"xnT{i}", BF16, [8, 512]) for i in range(2)]
            lat = alloc("lat", F32, [4, 416])
            clat = alloc("clat", BF16, [4, 384])
            kst = alloc("kst", BF16, [4, 96])
            gt = [alloc(f"gt{i}", F32, [512]) for i in range(4)]
            rt = alloc("rt", F32, [2, 64])

            for i in range(4):
                memset("pool", xl[i][:, 0:2], 0.0)
                memset("pool", xl[i][:, T + 2:T + 4], 0.0)
            memset("pool", kst[:, :, 0:64], 0.0)
            SS = stats[:, 0:4]
            RS = stats[:, 4:8]
            TM = stats[:, 8:16]
            SSQ = stats[:, 16:24]
            RSQ = stats[:, 24:32]
            def m1_xprep_ab(j):
                for b in range(4):
                    blk = j * 4 + b
                    dma(xt[b], x[s, blk * 128:(blk + 1) * 128, :])
                    act(xn[:, b, :], xt[b], AF.Square, accum=SS[:, b:b + 1])
                rstd_from_ss(RS[:, 0:4], SS[:, 0:4], stats[:, 80:84], 1.0 / D)
                for b in range(4):
                    stt(xn[:, b, :], xt[b], RS[:, b:b + 1], BC, ALU.mult, ALU.mult)

            def m1_xprep_t(j):
                xT = xnT[j % 2]
                for b in range(4):
                    bk = next_bank()
                    pb = psb_bf(bk)
                    for c in range(8):
                        tr(pb[:, c * 128:(c + 1) * 128], xn[:, b, c * 128:(c + 1) * 128], ident)
                    cp("act" if b % 2 else "dve", xT[:, :, b * 128:(b + 1) * 128],
                       pb.rearrange("p (c t) -> p c t", c=8))

            def m1_fm(j):
                xT = xnT[j % 2]
                for oc in range(8):
                    bk = next_bank()
                    pf = psb(bk)
                    for k in range(8):
                        mm(pf, w_in_b[:, k, oc * 128:(oc + 1) * 128], xT[:, k, :], k == 0, k == 7)
                    if oc < 4:
                        cp("act", xl[oc][:, 2 + j * 512:2 + (j + 1) * 512], pf)
                    else:
                        g0 = gt[(oc % 2) * 2]
                        g1 = gt[(oc % 2) * 2 + 1]
                        act(g0, pf, AF.Square)
                        stt(g0, g0, GC1 / GC2, pf, ALU.add, ALU.mult)
                        act(g1, g0, AF.Sigmoid, scale=GC2)
                        tt("dve", gg[:, oc - 4, j * 512:(j + 1) * 512], g1, pf, ALU.mult)

            def m1_tok_head(j):
                xT = xnT[j % 2]
                for b in range(4):
                    bk = next_bank()
                    pt = psb(bk)[:, 0:416]
                    for k in range(8):
                        mm(pt, xT[:, k, b * 128:(b + 1) * 128], w_in_b[:, k, 1024:1440], k == 0, k == 7)
                    cp("act", lat[:, b, :], pt)
                    act(clat[:, b, 0:256], lat[:, b, 0:256], AF.Square, accum=SSQ[:, b:b + 1])
                    act(clat[:, b, 256:384], lat[:, b, 256:384], AF.Square, accum=SSQ[:, 4 + b:5 + b])
                act(TM[:, 0:4], SSQ[:, 0:4], AF.Sqrt, bias=EPSC, scale=1.0 / 256)
                act(TM[:, 4:8], SSQ[:, 4:8], AF.Sqrt, bias=EPSC, scale=1.0 / 128)
                recip(RSQ[:, 0:8], TM[:, 0:8])
                for b in range(4):
                    blk = j * 4 + b
                    stt(clat[:, b, 0:256], lat[:, b, 0:256], RSQ[:, b:b + 1], CST[:, C_GQKVB:C_GQKVB + 256], ALU.mult, ALU.mult)
                    stt(clat[:, b, 256:384], lat[:, b, 256:384], RSQ[:, 4 + b:5 + b], CST[:, C_GQKVB + 256:C_GQKVB + 384], ALU.mult, ALU.mult)
                    kr = lat[:, b, 384:416].rearrange("p (a b) -> p a b", a=2)
                    cosb = CST[:, C_ROPE + blk * 32:C_ROPE + blk * 32 + 16].unsqueeze(1).to_broadcast([128, 2, 16])
                    sinb = CST[:, C_ROPE + blk * 32 + 16:C_ROPE + blk * 32 + 32].unsqueeze(1).to_broadcast([128, 2, 16])
                    t1 = rt[:, 0, 0:32].rearrange("p (a b) -> p a b", a=2)
                    t2 = rt[:, 1, 0:32].rearrange("p (a b) -> p a b", a=2)
                    tt("pool", t1, kr, cosb, ALU.mult)
                    tt("pool", t2, kr, sinb, ALU.mult)
                    tt("pool", kst[:, b, 64:80], t1[:, 0, :], t2[:, 1, :], ALU.subtract)
                    tt("pool", kst[:, b, 80:96], t1[:, 1, :], t2[:, 0, :], ALU.add)
                if j == 0:
                    load_w_rest()

            def m1_tok_tail(j):
                for b in range(4):
                    blk = j * 4 + b
                    bk = next_bank()
                    pb = psb_bf(bk)
                    tr(pb[:, 0:128], clat[:, b, 0:128], ident)
                    tr(pb[:, 128:256], clat[:, b, 128:256], ident)
                    tr(pb[:, 256:384], clat[:, b, 256:384], ident)
                    tr(pb[0:96, 384:512], kst[:, b, :], ident)
                    tok = slice(blk * 128, (blk + 1) * 128)
                    cp("dve", cqT[:, :, tok], pb[:, 0:256].rearrange("p (c t) -> p c t", c=2))
                    cp("dve", ckvT[:, tok], pb[:, 256:384])
                    cp("dve", kpeT[64:96, tok], pb[64:96, 384:512])

            dma(BC, cst_d[:, C_GMIXB:C_GMIXB + D])
            m1_xprep_ab(0)
            m1_xprep_t(0)
            for j in range(NTILE):
                if j + 1 < NTILE:
                    m1_xprep_ab(j + 1)
                m1_fm(j)
                if j >= 1:
                    m1_tok_tail(j - 1)
                if j + 1 < NTILE:
                    m1_xprep_t(j + 1)
                m1_tok_head(j)
            m1_tok_tail(NTILE - 1)
            for nm in ("xt0", "xt1", "xt2", "xt3", "xn", "xnT0", "xnT1", "lat", "clat", "kst", "gt0", "gt1", "gt2", "gt3", "rt", "w_in_b"):
                A.release(nm)

            _chk("M1")
            xcs = [alloc(f"xc{i}", F32, [T]) for i in range(2)]
            xcbs = [alloc(f"xcb{i}", BF16, [T]) for i in range(2)]
            HT = T // 2
            sets = [(alloc(f"ra{i}", F32, [HT]), alloc(f"a2{i}", F32, [HT]), alloc(f"iu{i}", F32, [HT])) for i in range(3)]
            lit = [0]
            h0 = alloc("h0", F32, [T])
            h1b = alloc("h1b", F32, [T])
            hs = [h0, h1b]

            def lru_conv(c):
                xc_ = xcs[c % 2]
                cw = lambda k: CST[:, C_CW + c * 4 + k:C_CW + c * 4 + k + 1]
                act(xc_, xl[c][:, 0:T], AF.Identity, bias=CST[:, C_CB + c:C_CB + c + 1], scale=cw(0))
                for k in range(1, 4):
                    stt(xc_, xl[c][:, k:k + T], cw(k), xc_, ALU.mult, ALU.add)
                cp("dve", xcbs[c % 2], xc_)
                A.release(f"xl{c}")

            lru_conv(0)
            for c in range(4):
                xc = xcs[c % 2]
                xcb = xcbs[c % 2]
                if c + 1 < 4:
                    lru_conv(c + 1)
                if c == 0:
                    sets.append((alloc("ra3", F32, [HT]), alloc("a23", F32, [HT]), alloc("iu3", F32, [HT])))
                for pair in (((0, 0), (1, 1)), ((0, 1), (1, 0))):
                    psets = [sets[(lit[0] + q) % len(sets)] for q in range(2)]
                    lit[0] += 2
                    for it, (d, hf) in enumerate(pair):
                        base = 4 * it
                        for gi in range(2):
                            for t in range(2):
                                mm(psb(base + gi * 2 + t), gate_b[:, c * 4 + d * 2 + gi, :],
                                   xcb[:, hf * HT + t * 512:hf * HT + (t + 1) * 512], True, True)
                    for it, (d, hf) in enumerate(pair):
                        ra_, a2_, iu_ = psets[it]
                        col = d * 4 + c
                        base = 4 * it
                        act(ra_.rearrange("p (a b) -> p a b", a=2), PS[:, base:base + 2, :], AF.Tanh, bias=GBAH[:, col:col + 1], scale=0.5)
                        act(iu_.rearrange("p (a b) -> p a b", a=2), PS[:, base + 2:base + 4, :], AF.Tanh, bias=GBXH[:, col:col + 1], scale=0.5)
                    for it, (d, hf) in enumerate(pair):
                        ra_, a2_, iu_ = psets[it]
                        col = d * 4 + c
                        act(a2_, ra_, AF.Exp, bias=COEF1[:, col:col + 1], scale=COEF1[:, col:col + 1])
                        act(ra_, ra_, AF.Exp, bias=COEFH[:, col:col + 1], scale=COEFH[:, col:col + 1])
                    for it, (d, hf) in enumerate(pair):
                        ra_, a2_, iu_ = psets[it]
                        act(a2_, a2_, AF.Sqrt, bias=QTR, scale=-0.25)
                    for it, (d, hf) in enumerate(pair):
                        ra_, a2_, iu_ = psets[it]
                        tsl = slice(hf * HT, (hf + 1) * HT)
                        stt(iu_, iu_, 1.0, xc[:, tsl], ALU.add, ALU.mult)
                        tt("dve", iu_, iu_, a2_, ALU.mult)
                        h = hs[d]
                        if d == 0:
                            init = 0.0 if hf == 0 else h[:, HT - 1:HT]
                            scan(h[:, tsl], ra_, iu_, init)
                        else:
                            init = 0.0 if hf == 1 else h[:, HT:HT + 1]
                            scan(h[:, tsl][:, ::-1], ra_[:, ::-1], iu_[:, ::-1], init)
                tt("dve", h0, h0, h1b, ALU.add)
                stt(yTl[:, c, :], h0, CST[:, C_GGRP + c:C_GGRP + c + 1], gg[:, c, :], ALU.mult, ALU.mult)
            for nm in ("xc0", "xc1", "xcb0", "xcb1", "ra0", "a20", "iu0", "ra1", "a21", "iu1", "ra2", "a22", "iu2", "ra3", "a23", "iu3", "h0", "h1b", "gg", "gate_b"):
                A.release(nm)

            _chk("M3")
            QT = alloc("QT", BF16, [8, T])
            KT = alloc("KT", BF16, [8, T])
            VA = alloc("VA", BF16, [NB * 8, 128])
            qsts = [alloc(f"qst{i}", BF16, [8, 96]) for i in range(2)]
            qt1s = [alloc(f"qt1{i}", F32, [8, 32]) for i in range(2)]
            qt2s = [alloc(f"qt2{i}", F32, [8, 32]) for i in range(2)]
            memset("dve", VA[:, :, 64:128], 1.0)
            for h in range(8):
                dma(KT[64:96, h, :], kpeT[64:96, :])

            def m2_head(blk):
                tok = slice(blk * 128, (blk + 1) * 128)
                qst, qt1, qt2 = qsts[blk % 2], qt1s[blk % 2], qt2s[blk % 2]
                bk = next_bank(2)
                for n in range(2):
                    for k in range(2):
                        mm(psb(bk + n)[:, 0:384], cqT[:, k, tok], w_uq_b[:, k, n * 384:(n + 1) * 384], k == 0, k == 1)
                cosq = CST[:, C_ROPEQ + blk * 32:C_ROPEQ + blk * 32 + 16]
                sinq = CST[:, C_ROPEQ + blk * 32 + 16:C_ROPEQ + blk * 32 + 32]
                for n in range(2):
                    q4 = psb(bk + n)[:, 0:384].rearrange("p (h d) -> p h d", h=4)
                    P.add("act", (lambda q4=q4, n=n, qst=qst: (lambda e: e.mul(out=qst[:, n * 4:(n + 1) * 4, 0:64], in_=q4[:, :, 0:64], mul=QSCALE)))(),
                          reads=[q4[:, :, 0:64]], writes=[qst[:, n * 4:(n + 1) * 4, 0:64]])
                    qpe = q4[:, :, 64:96].rearrange("p h (a b) -> p h a b", a=2)
                    cb_ = cosq.unsqueeze(1).unsqueeze(1).to_broadcast([128, 4, 2, 16])
                    sb_ = sinq.unsqueeze(1).unsqueeze(1).to_broadcast([128, 4, 2, 16])
                    tt("dve", qt1[:, n * 4:(n + 1) * 4, :].rearrange("p h (a b) -> p h a b", a=2), qpe, cb_, ALU.mult)
                    tt("dve", qt2[:, n * 4:(n + 1) * 4, :].rearrange("p h (a b) -> p h a b", a=2), qpe, sb_, ALU.mult)
                tt("pool", qst[:, :, 64:80], qt1[:, :, 0:16], qt2[:, :, 16:32], ALU.subtract)
                tt("pool", qst[:, :, 80:96], qt1[:, :, 16:32], qt2[:, :, 0:16], ALU.add)
                bk = next_bank()
                mm(psb(bk), ckvT[:, tok], w_ukv_b[:, :, 64:128], True, True)
                tt("dve", VA[:, blk * 8:(blk + 1) * 8, 0:64], psb(bk).rearrange("p (h d) -> p h d", h=8),
                   CST[:, C_GMLAB:C_GMLAB + 512].rearrange("p (h d) -> p h d", h=8), ALU.mult)

            def m2_tail(blk):
                tok = slice(blk * 128, (blk + 1) * 128)
                qst = qsts[blk % 2]
                bk = next_bank()
                pb = psb_bf(bk)
                for h in range(8):
                    tr(pb[0:96, h * 128:(h + 1) * 128], qst[:, h, :], ident)
                cp("act", QT[0:96, :, tok], pb[0:96, :].rearrange("p (h t) -> p h t", h=8))

            m2_head(0)
            for blk in range(NB):
                if blk + 1 < NB:
                    m2_head(blk + 1)
                m2_tail(blk)
            for t in range(NTILE):
                for h in range(8):
                    bk = next_bank()
                    mm(psb(bk)[0:64, :], w_ukv_b[:, h, 0:64], ckvT[:, t * 512:(t + 1) * 512], True, True)
                    cp("act" if h % 2 else "dve", KT[0:64, h, t * 512:(t + 1) * 512], psb(bk)[0:64, :])
            for nm in ("qst0", "qst1", "qt10", "qt11", "qt20", "qt21", "cqT", "ckvT", "kpeT", "w_uq_b", "w_ukv_b"):
                A.release(nm)

            _chk("M2")
            yTm = alloc("yTm", BF16, [4, T], top=True)
            PT = [alloc(f"PT{i}", BF16, [1024]) for i in range(4)]
            rden = alloc("rden", F32, [1024])
            it = 0
            for h in range(8):
                for qh in range(2):
                    obk = 4 + 2 * (it % 2)
                    q0 = qh * 1024

                    def s_mm(kb, sb0):
                        for t in range(2):
                            mm(psb(sb0 + t), KT[0:96, h, kb * 128:(kb + 1) * 128],
                               QT[0:96, h, q0 + t * 512:q0 + (t + 1) * 512], True, True)
                    s_mm(0, 0)
                    for kb in range(NB):
                        sb0 = 2 * (kb % 2)
                        if kb + 1 < NB:
                            s_mm(kb + 1, 2 * ((kb + 1) % 2))
                        pt = PT[kb % 4]
                        act(pt.rearrange("p (a b) -> p a b", a=2), PS[:, sb0:sb0 + 2, :], AF.Exp)
                        for t in range(2):
                            mm(psb(obk + t), VA[:, kb * 8 + h, :], pt[:, t * 512:(t + 1) * 512], kb == 0, kb == NB - 1)
                    ob = PS[:, obk:obk + 2, :]
                    recip(rden[0:64, :].rearrange("p (a b) -> p a b", a=2), ob[64:128, :, :])
                    po = (h % 2) * 64
                    tt("dve", yTm[po:po + 64, h // 2, q0:q0 + 1024].rearrange("p (a b) -> p a b", a=2),
                       ob[0:64, :, :], rden[0:64, :].rearrange("p (a b) -> p a b", a=2), ALU.mult)
                    it += 1
            for nm in ("PT0", "PT1", "PT2", "PT3", "rden", "QT", "KT", "VA"):
                A.release(nm)

            _chk("M4")
            hnT = alloc("hnT", BF16, [8, T])
            NJA = 20
            wupb = [alloc(f"wupb{i}", BF16, [8, 256]) for i in range(2)]
            wdbA = alloc("wdbA", BF16, [NJA, D])
            w_up_v = w_up_d.rearrange("(c p) n -> p c n", p=128)

            def load_wup(j):
                wb = wupb[j % 2]
                for part in range(2):
                    col = part * DFF + j * 128
                    cdma(wb[:, :, part * 128:(part + 1) * 128], w_up_v[:, :, col:col + 128])

            load_wup(0)
            load_wup(1)
            for j in range(NJA):
                cdma(wdbA[:, j, :], w_dn_d[j * 128:(j + 1) * 128, :])
            sqs = [alloc(f"sq{i}", BF16, [8, 512]) for i in range(2)]
            gst = alloc("gst", F32, [NB * 2])
            grs = alloc("grs", F32, [NB * 2])
            xr = [alloc(f"xr{i}", F32, [D]) for i in range(2)]
            h1t = [alloc(f"h1t{i}", F32, [D]) for i in range(2)]
            hn = alloc("hn", BF16, [D])
            junk = alloc("junk", BF16, [D])
            sbk = next_bank()
            for jt in range(NTILE):
                sq4 = sqs[jt % 2]
                tok4 = slice(jt * 512, (jt + 1) * 512)
                tt("dve", sq4[:, 0:4, :], yTl[:, :, tok4], yTl[:, :, tok4], ALU.mult)
                tt("dve", sq4[:, 4:8, :], yTm[:, :, tok4], yTm[:, :, tok4], ALU.mult)
                for b4 in range(4):
                    blk = jt * 4 + b4
                    for g in range(2):
                        col = blk * 2 + g
                        for c in range(4):
                            mm(psb(sbk)[:, col:col + 1], sq4[:, g * 4 + c, b4 * 128:(b4 + 1) * 128],
                               ig2[:, g * 4 + c:g * 4 + c + 1], c == 0, c == 3)
            cp("dve", gst, psb(sbk)[:, 0:NB * 2])
            act(gst, gst, AF.Sqrt, bias=EPSC, scale=1.0 / 512)
            recip(grs, gst)
            H1S = stats[:, 32:48]
            H1R = stats[:, 48:64]
            H1T = stats[:, 64:80]
            def m5_mm_half(i):
                blk, n = divmod(i, 2)
                tok = slice(blk * 128, (blk + 1) * 128)
                if n == 0:
                    dma(xr[blk % 2], x[s, tok, :])
                gb = 2 * (i % 3)
                for g in range(2):
                    for c in range(4):
                        mm(psb(gb + g), (yTl if g == 0 else yTm)[:, c, tok], w_o_b[:, g * 4 + c, n * 512:(n + 1) * 512], c == 0, c == 3)

            def m5_stt_half(i):
                blk, n = divmod(i, 2)
                xb = xr[blk % 2]
                hb = h1t[blk % 2]
                gb = 2 * (i % 3)
                stt(hb[:, n * 512:(n + 1) * 512], psb(gb), grs[:, blk * 2:blk * 2 + 1],
                    xb[:, n * 512:(n + 1) * 512], ALU.mult, ALU.add)
                stt(hb[:, n * 512:(n + 1) * 512], psb(gb + 1), grs[:, blk * 2 + 1:blk * 2 + 2],
                    hb[:, n * 512:(n + 1) * 512], ALU.mult, ALU.add)

            def m5_b(blk):
                tok = slice(blk * 128, (blk + 1) * 128)
                hb = h1t[blk % 2]
                rstd_from_ss(H1R[:, blk:blk + 1], H1S[:, blk:blk + 1], H1T[:, blk:blk + 1], 1.0 / D)
                stt(hn, hb, H1R[:, blk:blk + 1], BC, ALU.mult, ALU.mult)
                pb = psb_bf(6 + blk % 2)
                for c in range(8):
                    tr(pb[:, c * 128:(c + 1) * 128], hn[:, c * 128:(c + 1) * 128], ident)
                cp("act", hnT[:, :, tok], pb.rearrange("p (c t) -> p c t", c=8))

            dma(BC, cst_d[:, C_GFFN:C_GFFN + D])
            m5_mm_half(0)
            m5_mm_half(1)
            for i in range(2 * NB):
                blk, n = divmod(i, 2)
                tok = slice(blk * 128, (blk + 1) * 128)
                if i + 2 < 2 * NB:
                    m5_mm_half(i + 2)
                m5_stt_half(i)
                if n == 1:
                    hb = h1t[blk % 2]
                    dma(out[s, tok, :], hb)
                    act(junk, hb, AF.Square, accum=H1S[:, blk:blk + 1])
                elif blk >= 1:
                    m5_b(blk - 1)
            m5_b(NB - 1)
            bank_rr[0] = 0
            for nm in ("sq0", "sq1", "gst", "grs", "xr0", "xr1", "h1t0", "h1t1", "hn", "junk", "yTl", "yTm", "w_o_b"):
                A.release(nm)

            _chk("M5")
            actT = alloc("actT", BF16, [NJ, T])
            accg = alloc("accg", F32, [T])
            accv = alloc("accv", F32, [T])
            sgl = alloc("sgl", F32, [T])
            for j in range(NJ):
                wb = wupb[j % 2]
                if 1 <= j and j + 1 < NJ:
                    load_wup(j + 1)
                for part in range(2):
                    b0 = part * 4
                    for t in range(4):
                        for k in range(8):
                            mm(psb(b0 + t), wb[:, k, part * 128:(part + 1) * 128], hnT[:, k, t * 512:(t + 1) * 512], k == 0, k == 7)
                    jj = part * NJ + j
                    fw = lambda k: CST[:, C_FW + jj * 3 + k:C_FW + jj * 3 + k + 1]
                    acc = accg if part == 0 else accv
                    u = PS[:, b0:b0 + 4, :].rearrange("p a b -> p (a b)")
                    act(acc.rearrange("p (a b) -> p a b", a=4), PS[:, b0:b0 + 4, :], AF.Identity,
                        bias=CST[:, C_FB + jj:C_FB + jj + 1], scale=fw(1))
                    stt(acc[:, 1:T], u[:, 0:T - 1], fw(0), acc[:, 1:T], ALU.mult, ALU.add)
                    stt(acc[:, 0:T - 1], u[:, 1:T], fw(2), acc[:, 0:T - 1], ALU.mult, ALU.add)
                act(sgl, accg, AF.Silu)
                tt("dve", actT[:, j, :], sgl, accv, ALU.mult)
            for nm in ("wupb0", "wupb1", "accg", "accv", "sgl", "hnT"):
                A.release(nm)

            if s + 1 < NSEQ:
                w_in_next[0] = alloc("w_in_b", BF16, [8, INC])
            dma(BC, cst_d[:, C_FG:C_FG + D])
            wdbB = alloc("wdbB", BF16, [NJ - NJA, D])
            for j in range(NJA, NJ):
                cdma(wdbB[:, j - NJA, :], w_dn_d[j * 128:(j + 1) * 128, :])
            hr = [alloc(f"hr{i}", F32, [D]) for i in range(2)]
            yo = [alloc(f"yo{i}", F32, [D]) for i in range(2)]
            junk = alloc("junk", BF16, [D])
            FS = stats[:, 32:48]
            FR = stats[:, 48:64]
            FT = stats[:, 64:80]
            for blk in range(NB):
                tok = slice(blk * 128, (blk + 1) * 128)
                hb = hr[blk % 2]
                yb = yo[blk % 2]
                dma(hb, out[s, tok, :])
                bk = next_bank(2)
                for n in range(2):
                    for j in range(NJ):
                        mm(psb(bk + n), actT[:, j, tok], (wdbA[:, j, n * 512:(n + 1) * 512] if j < NJA else wdbB[:, j - NJA, n * 512:(n + 1) * 512]), j == 0, j == NJ - 1)
                for n in range(2):
                    tt("dve", yb[:, n * 512:(n + 1) * 512], psb(bk + n), hb[:, n * 512:(n + 1) * 512], ALU.add)
                act(junk, yb, AF.Square, accum=FS[:, blk:blk + 1])
                rstd_from_ss(FR[:, blk:blk + 1], FS[:, blk:blk + 1], FT[:, blk:blk + 1], 1.0 / D)
                stt(yb, yb, FR[:, blk:blk + 1], BC, ALU.mult, ALU.mult)
                dma(out[s, tok, :], yb)
                if blk == 3 and s + 1 < NSEQ:
                    load_w_in(w_in_next[0])
            for nm in ("wdbA", "wdbB", "hr0", "hr1", "yo0", "yo1", "junk", "actT"):
                A.release(nm)

        P.finalize()
        import contextlib
        with contextlib.ExitStack() as es:
            csems = {e: es.enter_context(nc.semaphore(f"c_{e}")) for e in Prog.CE}
            dsems = [es.enter_context(nc.semaphore(f"d_{i}")) for i in range(P.ndsem)]
            block = es.enter_context(nc.Block())

            @block.sync
            def _(eng):
                P.emit_engine("sp", eng, csems, dsems)

            @block.tensor
            def _(eng):
                P.emit_engine("pe", eng, csems, dsems)

            @block.scalar
            def _(eng):
                P.emit_engine("act", eng, csems, dsems)

            @block.vector
            def _(eng):
                P.emit_engine("dve", eng, csems, dsems)

            @block.gpsimd
            def _(eng):
                P.emit_engine("pool", eng, csems, dsems)
    return nc


def _pc(v, nchunk):
    return np.ascontiguousarray(np.asarray(v, np.float32).reshape(nchunk, 128).T)


def _build_consts(inp):
    cst = np.zeros((128, NCST), np.float32)
    cw = np.asarray(inp["lru_conv_w"], np.float32)[0]
    cst[:, C_CW:C_CW + 16] = cw.reshape(4, 4, 128).transpose(2, 1, 0).reshape(128, 16)
    cst[:, C_CB:C_CB + 4] = _pc(np.asarray(inp["lru_conv_b"])[0], 4)
    for name, off in (("lru_gate_a_b", C_GBA), ("lru_gate_x_b", C_GBX), ("lru_lambda", C_LAM)):
        v = np.asarray(inp[name], np.float32)[0]
        cst[:, off:off + 8] = v.reshape(2, 4, 128).transpose(2, 0, 1).reshape(128, 8)
    fw = np.asarray(inp["ffn_conv_w"], np.float32)[0]
    cst[:, C_FW:C_FW + 132] = fw.reshape(3, 44, 128).transpose(2, 1, 0).reshape(128, 132)
    cst[:, C_FB:C_FB + 44] = _pc(np.asarray(inp["ffn_conv_b"])[0], 44)
    cst[:, C_GMIX:C_GMIX + 8] = _pc(np.asarray(inp["ln_mix_g"])[0], 8)
    cst[:, C_GQ:C_GQ + 2] = _pc(np.asarray(inp["q_norm_g"])[0], 2)
    cst[:, C_GKV:C_GKV + 1] = _pc(np.asarray(inp["kv_norm_g"])[0], 1)
    ggrp = np.concatenate([np.asarray(inp["grp_norm_lru_g"], np.float32)[0], np.asarray(inp["grp_norm_mla_g"], np.float32)[0]])
    cst[:, C_GGRP:C_GGRP + 8] = _pc(ggrp, 8)
    half = 16
    freqs = (np.float32(10000.0) ** (-(np.arange(half, dtype=np.float32) / np.float32(half)))).astype(np.float32)
    pos = np.arange(T, dtype=np.float32)
    ang = (pos[:, None] * freqs[None, :]).astype(np.float32)
    cs = np.concatenate([np.cos(ang), np.sin(ang)], axis=1).astype(np.float32)
    tab = cs.reshape(NB, 128, 32).transpose(1, 0, 2).reshape(128, NB * 32)
    cst[:, C_ROPE:C_ROPE + 512] = tab
    cst[:, C_ROPEQ:C_ROPEQ + 512] = tab * np.float32(QSCALE)
    cst[:, C_FG:C_FG + D] = np.broadcast_to(np.asarray(inp["final_norm_g"], np.float32)[None, :], (128, D))
    cst[:, C_GFFN:C_GFFN + D] = np.broadcast_to(np.asarray(inp["ln_ffn_g"], np.float32)[0][None, :], (128, D))
    cst[:, C_IDENT:C_IDENT + 128] = np.eye(128, dtype=np.float32)
    cst[:, C_GMIXB:C_GMIXB + D] = np.broadcast_to(np.asarray(inp["ln_mix_g"], np.float32)[0][None, :], (128, D))
    gqkv = np.concatenate([np.asarray(inp["q_norm_g"], np.float32)[0], np.asarray(inp["kv_norm_g"], np.float32)[0]])
    cst[:, C_GQKVB:C_GQKVB + 384] = np.broadcast_to(gqkv[None, :], (128, 384))
    cst[:, C_GMLAB:C_GMLAB + 512] = np.broadcast_to(np.asarray(inp["grp_norm_mla_g"], np.float32)[0][None, :], (128, 512))
    return cst


def _build_gates(inp):
    ga = np.asarray(inp["lru_gate_a_w"], np.float32)[0]
    gx = np.asarray(inp["lru_gate_x_w"], np.float32)[0]
    g = np.zeros((128, 4, 4, 128), np.float32)
    for c in range(4):
        for d in range(2):
            for gi, w in enumerate((ga, gx)):
                for hh in range(2):
                    g[hh * 64:(hh + 1) * 64, c, d * 2 + gi, hh * 64:(hh + 1) * 64] = w[d, 2 * c + hh]
    return g.reshape(128, 2048)


_NC_CACHE = {}


def kernel(**inputs):
    inp = {k: np.asarray(v) for k, v in inputs.items()}
    n = 8
    xfull = np.ascontiguousarray(inp["x"], dtype=np.float32)
    cst = _build_consts(inp)
    gates = _build_gates(inp)
    shared = {
        "cst": cst,
        "w_in": np.ascontiguousarray(inp["w_in"][0], np.float32),
        "gates": gates,
        "w_uq": np.ascontiguousarray(inp["w_uq"][0].reshape(256, 768), np.float32),
        "w_ukv": np.ascontiguousarray(inp["w_ukv"][0].reshape(128, 1024), np.float32),
        "w_o": np.ascontiguousarray(inp["w_o"][0], np.float32),
        "w_up": np.ascontiguousarray(inp["w_up"][0], np.float32),
        "w_down": np.ascontiguousarray(inp["w_down"][0], np.float32),
    }
    if "nc" not in _NC_CACHE:
        _NC_CACHE["nc"] = build_program()
    nc = _NC_CACHE["nc"]
    in_maps = []
    for c in range(n):
        m = dict(shared)
        m["x"] = np.ascontiguousarray(xfull[c * NSEQ:(c + 1) * NSEQ])
        in_maps.append(m)
    res = run_bass_kernel_spmd(nc, in_maps, core_ids=list(range(n)))
    outs = [np.asarray(r["out"], dtype=np.float32) for r in res.results]
    return np.concatenate(outs, axis=0)
```
